# Optimizing a Trainium2 kernel written in Bass

```python
import math
import jax, jax.numpy as jnp
from jax import lax
import numpy as np

D_MODEL = 1024
BATCH = 16
SEQ = 4096
DEPTH = 4

MEM_LEN = 256
CONV_CH = D_MODEL // 4
CONV_WIDTH = 31
CONV_PAD = (CONV_WIDTH - 1) // 2
DIFF_HEADS = 4
DIFF_W = D_MODEL // 2
DIFF_DV = DIFF_W // DIFF_HEADS
DIFF_DH = DIFF_DV // 2
DIFF_QK_W = DIFF_HEADS * 2 * DIFF_DH
MEM_HEADS = 4
MEM_W = D_MODEL // 4
MEM_DH = MEM_W // MEM_HEADS
MIX_W = CONV_CH + DIFF_W + MEM_W
IN_W = 2 * CONV_CH + 2 * DIFF_QK_W + DIFF_W + MEM_W
D_FF = 4 * D_MODEL
N_BUCKETS = 32
MAX_DISTANCE = 128
Q_BLOCK = 128
ALPHA = (2.0 * DEPTH) ** 0.25
BETA = (8.0 * DEPTH) ** -0.25
LN_EPS = 1e-5

kernel_name = "hybrid_conformer_diffattn_memory_encoder"


def layer_norm(x, g, b):
    xf = x.astype(jnp.float32)
    mu = jnp.mean(xf, axis=-1, keepdims=True)
    var = jnp.mean(jnp.square(xf - mu), axis=-1, keepdims=True)
    y = (xf - mu) * lax.rsqrt(var + LN_EPS) * g.astype(jnp.float32) + b.astype(jnp.float32)
    return y.astype(x.dtype)


def t5_bucket(rel):
    half = N_BUCKETS // 2
    max_exact = half // 2
    ret = (rel > 0).astype(jnp.int32) * half
    n = jnp.abs(rel)
    nf = jnp.maximum(n, 1).astype(jnp.float32)
    large = max_exact + (jnp.log(nf / max_exact) / math.log(MAX_DISTANCE / max_exact)
                         * (half - max_exact)).astype(jnp.int32)
    large = jnp.minimum(large, half - 1)
    return ret + jnp.where(n < max_exact, n, large)


def relative_bias_offsets(rel_bias, seq):
    offsets = jnp.arange(-(seq - 1), seq, dtype=jnp.int32)
    return rel_bias[t5_bucket(offsets)].T.astype(jnp.float32)


def conformer_conv(c_in, conv_w, conv_b, ln_g, ln_b):
    a, gate = jnp.split(c_in, 2, axis=-1)
    u = a * jax.nn.sigmoid(gate)
    y = lax.conv_general_dilated(
        u, conv_w[:, None, :], window_strides=(1,), padding=[(CONV_PAD, CONV_PAD)],
        dimension_numbers=("NWC", "WIO", "NWC"), feature_group_count=CONV_CH) + conv_b
    return jax.nn.silu(layer_norm(y, ln_g, ln_b))


def diff_attention(q, k, v, bias_off, lq1, lk1, lq2, lk2, lam_init, norm_g):
    B, S, _ = q.shape
    H, dh, dv = DIFF_HEADS, DIFF_DH, DIFF_DV
    q = q.reshape(B, S, H, 2, dh)
    k = k.reshape(B, S, H, 2, dh)
    q1 = q[..., 0, :].transpose(0, 2, 1, 3)
    q2 = q[..., 1, :].transpose(0, 2, 1, 3)
    k1 = k[..., 0, :].transpose(0, 2, 1, 3)
    k2 = k[..., 1, :].transpose(0, 2, 1, 3)
    vh = v.reshape(B, S, H, dv).transpose(0, 2, 1, 3)
    lam = (jnp.exp(jnp.sum(lq1.astype(jnp.float32) * lk1.astype(jnp.float32)))
           - jnp.exp(jnp.sum(lq2.astype(jnp.float32) * lk2.astype(jnp.float32)))
           + lam_init)
    scale = dh ** -0.5
    nb = S // Q_BLOCK
    kpos = jnp.arange(S, dtype=jnp.int32)

    def to_blocks(t):
        return t.reshape(B, H, nb, Q_BLOCK, t.shape[-1]).transpose(2, 0, 1, 3, 4)

    def block(args):
        q1b, q2b, start = args
        qpos = start + jnp.arange(Q_BLOCK, dtype=jnp.int32)
        bias = bias_off[:, kpos[None, :] - qpos[:, None] + S - 1]
        s1 = jnp.einsum("bhqd,bhkd->bhqk", q1b, k1).astype(jnp.float32) * scale + bias
        s2 = jnp.einsum("bhqd,bhkd->bhqk", q2b, k2).astype(jnp.float32) * scale + bias
        attn = jax.nn.softmax(s1, axis=-1) - lam * jax.nn.softmax(s2, axis=-1)
        return jnp.einsum("bhqk,bhkd->bhqd", attn.astype(vh.dtype), vh)

    starts = jnp.arange(nb, dtype=jnp.int32) * Q_BLOCK
    o = lax.map(block, (to_blocks(q1), to_blocks(q2), starts))
    o = o.transpose(1, 0, 3, 2, 4).reshape(B, S, H, dv)
    of = o.astype(jnp.float32)
    of = of * lax.rsqrt(jnp.mean(jnp.square(of), axis=-1, keepdims=True) + LN_EPS)
    of = of * norm_g.astype(jnp.float32) * (1.0 - lam_init)
    return of.astype(q.dtype).reshape(B, S, DIFF_W)


def memory_attention(qm, mem, w_mem_kv):
    B, S, _ = qm.shape
    M = mem.shape[1]
    qh = qm.reshape(B, S, MEM_HEADS, MEM_DH)
    km, vm = jnp.split(mem @ w_mem_kv, 2, axis=-1)
    km = km.reshape(B, M, MEM_HEADS, MEM_DH)
    vm = vm.reshape(B, M, MEM_HEADS, MEM_DH)
    s = jnp.einsum("bshd,bmhd->bhsm", qh, km).astype(jnp.float32) * (MEM_DH ** -0.5)
    p = jax.nn.softmax(s, axis=-1).astype(vm.dtype)
    return jnp.einsum("bhsm,bmhd->bshd", p, vm).reshape(B, S, MEM_W)


def hybrid_mixer(h, mem, bias_off, w_in, b_in, conv_w, conv_b, conv_ln_g, conv_ln_b,
                 lq1, lk1, lq2, lk2, lam_init, diff_norm_g, w_mem_kv, w_out, b_out):
    proj = h @ w_in + b_in
    s0 = 2 * CONV_CH
    s1 = s0 + DIFF_QK_W
    s2 = s1 + DIFF_QK_W
    s3 = s2 + DIFF_W
    c_in, q, k, v, qm = jnp.split(proj, [s0, s1, s2, s3], axis=-1)
    conv_out = conformer_conv(c_in, conv_w, conv_b, conv_ln_g, conv_ln_b)
    diff_out = diff_attention(q, k, v, bias_off, lq1, lk1, lq2, lk2, lam_init, diff_norm_g)
    mem_out = memory_attention(qm, mem, w_mem_kv)
    mixed = jnp.concatenate([conv_out, diff_out, mem_out], axis=-1)
    return mixed @ w_out + b_out


def setup_inputs(seed: int = 0) -> dict:
    key = jax.random.key(seed)
    ks = jax.random.split(key, 32)
    L, D = DEPTH, D_MODEL
    nrm = jax.random.normal
    f32 = jnp.float32
    return {
        "x": nrm(ks[0], (BATCH, SEQ, D), f32),
        "mem": nrm(ks[1], (BATCH, MEM_LEN, D), f32),
        "emb_ln_g": 1.0 + 0.02 * nrm(ks[2], (D,), f32),
        "emb_ln_b": 0.02 * nrm(ks[3], (D,), f32),
        "rel_bias": 0.5 * nrm(ks[4], (N_BUCKETS, DIFF_HEADS), f32),
        "w_in": nrm(ks[5], (L, D, IN_W), f32) * D ** -0.5,
        "b_in": 0.02 * nrm(ks[6], (L, IN_W), f32),
        "conv_w": nrm(ks[7], (L, CONV_WIDTH, CONV_CH), f32) * CONV_WIDTH ** -0.5,
        "conv_b": 0.02 * nrm(ks[8], (L, CONV_CH), f32),
        "conv_ln_g": 1.0 + 0.02 * nrm(ks[9], (L, CONV_CH), f32),
        "conv_ln_b": 0.02 * nrm(ks[10], (L, CONV_CH), f32),
        "lambda_q1": 0.1 * nrm(ks[11], (L, DIFF_DH), f32),
        "lambda_k1": 0.1 * nrm(ks[12], (L, DIFF_DH), f32),
        "lambda_q2": 0.1 * nrm(ks[13], (L, DIFF_DH), f32),
        "lambda_k2": 0.1 * nrm(ks[14], (L, DIFF_DH), f32),
        "diff_norm_g": 1.0 + 0.02 * nrm(ks[15], (L, DIFF_DV), f32),
        "w_mem_kv": nrm(ks[16], (L, D, 2 * MEM_W), f32) * D ** -0.5,
        "w_out": nrm(ks[17], (L, MIX_W, D), f32) * (MIX_W ** -0.5) * BETA,
        "b_out": 0.02 * nrm(ks[18], (L, D), f32),
        "ln1_g": 1.0 + 0.02 * nrm(ks[19], (L, D), f32),
        "ln1_b": 0.02 * nrm(ks[20], (L, D), f32),
        "w_up": nrm(ks[21], (L, D, D_FF), f32) * D ** -0.5,
        "w_down": nrm(ks[22], (L, D_FF, D), f32) * (D_FF ** -0.5) * BETA,
        "ln2_g": 1.0 + 0.02 * nrm(ks[23], (L, D), f32),
        "ln2_b": 0.02 * nrm(ks[24], (L, D), f32),
    }


def reference(x, mem, emb_ln_g, emb_ln_b, rel_bias, w_in, b_in, conv_w, conv_b,
              conv_ln_g, conv_ln_b, lambda_q1, lambda_k1, lambda_q2, lambda_k2,
              diff_norm_g, w_mem_kv, w_out, b_out, ln1_g, ln1_b, w_up, w_down,
              ln2_g, ln2_b):
    S = x.shape[1]
    x = layer_norm(x, emb_ln_g, emb_ln_b)
    bias_off = relative_bias_offsets(rel_bias, S)
    for l in range(DEPTH):
        lam_init = 0.8 - 0.6 * math.exp(-0.3 * l)
        mixed = hybrid_mixer(x, mem, bias_off, w_in[l], b_in[l], conv_w[l], conv_b[l],
                             conv_ln_g[l], conv_ln_b[l], lambda_q1[l], lambda_k1[l],
                             lambda_q2[l], lambda_k2[l], lam_init, diff_norm_g[l],
                             w_mem_kv[l], w_out[l], b_out[l])
        x = layer_norm(ALPHA * x + mixed, ln1_g[l], ln1_b[l])
        ff = jnp.square(jax.nn.relu(x @ w_up[l])) @ w_down[l]
        x = layer_norm(ALPHA * x + ff, ln2_g[l], ln2_b[l])
    return x
```

```python
import math
import os
from contextlib import ExitStack

import numpy as np
import concourse.bass as bass
import concourse.mybir as mybir
from concourse.bass_utils import run_bass_kernel_spmd

F32 = mybir.dt.float32
BF16 = mybir.dt.bfloat16
AF = mybir.ActivationFunctionType
ALU = mybir.AluOpType
AX = mybir.AxisListType

D = 1024
DEPTH = 4
MEM_LEN = 256
CONV_W = 31
IN_W = 2304
D_FF = 4096
ALPHA = (2.0 * DEPTH) ** 0.25
LN_EPS = 1e-5
NCORES = 8


class Buf:
    __slots__ = ("name", "w", "r", "excl", "sem")

    def __init__(self, name, excl=False):
        self.name = name
        self.w = None
        self.r = {}
        self.excl = excl
        self.sem = None


class Sched:
    def __init__(self, nc, es):
        self.nc = nc
        self.h = {"pe": nc.tensor, "act": nc.scalar, "dve": nc.vector, "pool": nc.gpsimd, "sp": nc.sync}
        self.sem = {e: es.enter_context(nc.semaphore("s_" + e)) for e in ("pe", "act", "dve", "pool")}
        self.cnt = {e: 0 for e in self.sem}
        self.seen = {e: {} for e in self.h}
        self.pending = {e: [] for e in self.sem}
        self.bar = es.enter_context(nc.semaphore("s_bar"))
        self.nbar = 0
        self.dsems = [es.enter_context(nc.semaphore(f"s_d{i}")) for i in range(90)]
        self.dval = [0] * len(self.dsems)
        self.dfree = {"pool": list(range(16)), "sp": list(range(16, len(self.dsems)))}
        self.dq = {}
        self.dbufs = []
        self.allbufs = []
        self.pool_out = []

    def buf(self, name, excl=False):
        b = Buf(name, excl)
        self.allbufs.append(b)
        return b

    def _semof(self, key):
        return self.sem[key] if isinstance(key, str) else self.dsems[key[1]]

    def _deps(self, eng, reads, writes):
        deps = {}

        def add(tok):
            if tok is None:
                return
            k, v = tok
            if deps.get(k, 0) < v:
                deps[k] = v

        for b in reads:
            add(b.w)
            if b.excl:
                for k, v in b.r.items():
                    if k != eng:
                        add((k, v))
        for b in writes:
            add(b.w)
            for k, v in b.r.items():
                add((k, v))
        return deps

    def _wait(self, eng, deps):
        seen = self.seen[eng]
        for k, v in deps.items():
            if k == eng and eng == "pe":
                continue
            if seen.get(k, 0) >= v:
                continue
            self.h[eng].wait_ge(self._semof(k), v)
            seen[k] = v

    def _mark(self, tok, reads, writes):
        k, v = tok
        for b in reads:
            if b.r.get(k, 0) < v:
                b.r[k] = v
        for b in writes:
            b.w = tok
            b.r = {}

    def op(self, eng, fn, reads=(), writes=(), signal=True):
        deps = self._deps(eng, reads, writes)
        self._wait(eng, deps)
        ins = fn()
        if not signal:
            self.pending[eng].append((list(reads), list(writes)))
            return None
        ins.then_inc(self.sem[eng], 1)
        self.cnt[eng] += 1
        tok = (eng, self.cnt[eng])
        for r_, w_ in self.pending[eng]:
            self._mark(tok, r_, w_)
        self.pending[eng] = []
        self._mark(tok, reads, writes)
        return tok

    def dma(self, q, out, in_, reads=(), writes=(), sembuf=None):
        deps = self._deps(q, reads, writes)
        self._wait(q, deps)
        if sembuf.sem is None:
            assert self.dfree[q], "out of dma semaphores"
            sembuf.sem = self.dfree[q].pop(0)
            self.dq[sembuf.sem] = q
            self.dbufs.append(sembuf)
        assert self.dq[sembuf.sem] == q, "buffer DMA'd from two queues"
        ins = self.h[q].dma_start(out=out, in_=in_)
        ins.then_inc(self.dsems[sembuf.sem], 16)
        self.dval[sembuf.sem] += 16
        tok = (("d", sembuf.sem), self.dval[sembuf.sem])
        self._mark(tok, reads, writes)
        if q == "pool":
            self.pool_out.append(tok)
            if len(self.pool_out) > 3:
                k, v = self.pool_out.pop(0)
                self._wait("pool", {k: v})
        return tok

    def barrier(self):
        for e in self.pending:
            assert not self.pending[e], f"pending ops on {e} at barrier"
        sp = self.h["sp"]
        seen = self.seen["sp"]
        for e in self.sem:
            if seen.get(e, 0) < self.cnt[e]:
                sp.wait_ge(self.sem[e], self.cnt[e])
                seen[e] = self.cnt[e]
        for b in self.dbufs:
            k = ("d", b.sem)
            if seen.get(k, 0) < self.dval[b.sem]:
                sp.wait_ge(self.dsems[b.sem], self.dval[b.sem])
                seen[k] = self.dval[b.sem]
        sp.sem_inc(self.bar, 1)
        self.nbar += 1
        for e in self.sem:
            self.h[e].wait_ge(self.bar, self.nbar)
        for e in self.h:
            for k2 in self.sem:
                self.seen[e][k2] = self.cnt[k2]
            for i in range(len(self.dsems)):
                self.seen[e][("d", i)] = self.dval[i]
        for b in self.dbufs:
            self.dfree[self.dq[b.sem]].append(b.sem)
            b.sem = None
        self.dbufs = []
        self.pool_out = []
        for b in self.allbufs:
            b.w = None
            b.r = {}
        self.allbufs = [b for b in self.allbufs if b.excl]


def t5_bucket_np(rel):
    rel = np.asarray(rel, np.int64)
    half, max_exact = 16, 8
    ret = (rel > 0).astype(np.int64) * half
    n = np.abs(rel)
    nf = np.maximum(n, 1).astype(np.float32)
    large = max_exact + (np.log(nf / np.float32(max_exact)) / np.float32(math.log(128 / max_exact))
                         * np.float32(half - max_exact)).astype(np.int32)
    large = np.minimum(large, half - 1)
    return ret + np.where(n < max_exact, n, large)


class _Stop(Exception):
    pass


def build(S, L, NSEQ, dbg=False, stop=99):
    nc = bass.Bass("TRN2", target_bir_lowering=False)
    try:
        _build(nc, S, L, NSEQ, dbg, stop)
    except _Stop:
        pass
    return nc


def _build(nc, S, L, NSEQ, dbg, stop):
    NT = S // 128
    NB = S // 512
    SP = S + 30

    def din(name, shape, dt=F32):
        return nc.dram_tensor(name, shape, dt, kind="ExternalInput").ap()

    x_d = din("x", [NSEQ, S, D])
    mem_d = din("mem", [NSEQ, MEM_LEN, D])
    emb_d = din("emb", [128, 2, D])
    btab_d = din("btab", [128, 4, 1152])
    cfar_d = din("cfar", [128, 8])
    win_d = din("w_in", [L, D, IN_W])
    wkv_d = din("w_mem_kv", [L, D, 512])
    wout_d = din("w_out", [L, D, D])
    wup_d = din("w_up", [L, D, D_FF])
    wdn_d = din("w_down", [L, D_FF, D])
    bincol_d = din("bincol", [128, L, 18])
    bv_d = din("bvt", [128, L, 512])
    cwcol_d = din("cwcol", [128, L, 2, CONV_W])
    cvec_d = din("cvec", [128, L, 3, 256])
    lamv_d = din("lamv", [128, L, 4, 64])
    dng_d = din("dng", [128, L, 128])
    vec_d = din("vec", [128, L, 5, D])
    out_d = nc.dram_tensor("out", [NSEQ, S, D], F32, kind="ExternalOutput").ap()

    sk = "ExternalOutput" if dbg else "Internal"

    def dsc(name, shape, dt):
        return nc.dram_tensor(name, shape, dt, kind=sk).ap()

    xres_d = dsc("xres", [NSEQ, S, D], F32)
    x1res_d = dsc("x1res", [NSEQ, S, D], F32)
    xT_d = dsc("xT", [NSEQ, 8, 128, S], BF16)
    x1T_d = dsc("x1T", [NSEQ, 8, 128, S], BF16)
    QT_d = dsc("QT", [NSEQ, 4, 128, S], BF16)
    KT_d = dsc("KT", [NSEQ, 4, 128, S], BF16)
    V_d = dsc("V", [NSEQ, S, 512], BF16)
    QmT_d = dsc("QmT", [NSEQ, 2, 128, S], BF16)
    uT_d = dsc("uT", [NSEQ, 2, 128, SP], BF16)
    cv_d = dsc("cv", [NSEQ, S, 256], BF16)

    def cpt(ap):
        return ap.rearrange("c p t -> p c t")

    with ExitStack() as ges:
        sch = Sched(nc, ges)
        V, G, A, PE = nc.vector, nc.gpsimd, nc.scalar, nc.tensor

        uq = [0]

        def uniq(name):
            uq[0] += 1
            return f"{name}_{uq[0]}"

        def tile(es, name, shape, dt, nb=0, excl=False):
            t = es.enter_context(nc.sbuf_tensor(uniq(name), shape, dt))
            if nb == 0:
                return t, sch.buf(name, excl)
            return t, [sch.buf(f"{name}.{i}", excl) for i in range(nb)]

        psall = ges.enter_context(nc.psum_tensor("psall", [128, 4096], F32))
        PS = [psall[:, i * 512:(i + 1) * 512] for i in range(8)]
        PB = [sch.buf(f"ps{i}", excl=True) for i in range(8)]

        ident_f, identf_b = tile(ges, "ident_f", [128, 128], F32)
        ident, ident_b = tile(ges, "ident", [128, 128], BF16)
        epsc, epsc_b = tile(ges, "epsc", [128, 1], F32)
        zero_bf, zero_b = tile(ges, "zero_bf", [128, 32], BF16)
        scr, scr_b = tile(ges, "scr", [128, 8], F32)

        sch.op("pool", lambda: G.memset(ident_f[:], 1.0), writes=[identf_b])
        sch.op("pool", lambda: G.affine_select(out=ident_f[:], in_=ident_f[:], pattern=[[-1, 128]],
                                               compare_op=ALU.is_equal, fill=0.0, base=0, channel_multiplier=1),
               reads=[identf_b], writes=[identf_b])
        sch.op("dve", lambda: V.tensor_copy(ident[:], ident_f[:]), reads=[identf_b], writes=[ident_b])
        sch.op("dve", lambda: V.memset(epsc[:], LN_EPS), writes=[epsc_b])
        sch.op("dve", lambda: V.memset(zero_bf[:], 0.0), writes=[zero_b])
        for s in range(NSEQ):
            for c in range(2):
                sch.dma("sp", uT_d[s, c, :, 0:15], zero_bf[:, 0:15], reads=[zero_b], sembuf=zero_b)
                sch.dma("sp", uT_d[s, c, :, 15 + S:SP], zero_bf[:, 0:15], reads=[zero_b], sembuf=zero_b)

        class LNCtx:
            def __init__(self, es, N, pfx, nslot=2, inplace=False, xh_eng="act"):
                self.N = N
                self.ns = nslot
                self.k = 0
                self.xh_eng = xh_eng
                nch = max(1, N // 512)
                self.nch = nch
                self.z, self.zb = tile(es, pfx + "z", [128, nslot, N], F32, nslot)
                self.st = es.enter_context(nc.sbuf_tensor(uniq(pfx + "st"), [128, nslot, nch, 6], F32))
                self.stb = [[sch.buf(f"{pfx}st{i}{j}") for j in range(nch)] for i in range(nslot)]
                self.sm, self.smb = tile(es, pfx + "sm", [128, nslot, 8], F32, nslot)
                self.smb2 = [sch.buf(pfx + "sm2") for _ in range(nslot)]
                self.smb3 = [sch.buf(pfx + "sm3") for _ in range(nslot)]
                self.smb4 = [sch.buf(pfx + "sm4") for _ in range(nslot)]
                if inplace:
                    self.xh, self.xhb = self.z, self.zb
                else:
                    self.xh, self.xhb = tile(es, pfx + "xh", [128, nslot, N], F32, nslot)

            def slot(self):
                i = self.k % self.ns
                self.k += 1
                return i

            def stages(self, i, g_ap, b_ap, gb_bufs):
                N = self.N
                z = self.z[:, i, :]
                zb = self.zb[i]
                w = min(N, 512)
                st, sm = self.st, self.sm
                xh = self.xh[:, i, :]
                xhb = self.xhb[i]

                def s1():
                    for j in range(self.nch):
                        sch.op("dve", lambda: V.bn_stats(st[:, i, j, :], z[:, j * w:(j + 1) * w]),
                               reads=[zb], writes=[self.stb[i][j]])
                    sch.op("dve", lambda: V.bn_aggr(sm[:, i, 0:2], st[:, i, :, :]),
                           reads=self.stb[i], writes=[self.smb[i]])

                def s2():
                    sch.op("act", lambda: A.activation(out=sm[:, i, 2:3], in_=sm[:, i, 1:2], func=AF.Ln,
                                                       bias=epsc[:], scale=1.0),
                           reads=[self.smb[i], epsc_b], writes=[self.smb2[i]])
                    sch.op("act", lambda: A.activation(out=sm[:, i, 3:4], in_=sm[:, i, 2:3], func=AF.Exp,
                                                       scale=-0.5),
                           reads=[self.smb2[i]], writes=[self.smb3[i]])

                def s3():
                    sch.op("dve", lambda: V.scalar_tensor_tensor(out=sm[:, i, 4:5], in0=sm[:, i, 0:1], scalar=-1.0,
                                                                 in1=sm[:, i, 3:4], op0=ALU.mult, op1=ALU.mult),
                           reads=[self.smb[i], self.smb3[i]], writes=[self.smb4[i]])
                    if self.xh_eng == "act":
                        sch.op("act", lambda: A.activation(out=xh, in_=z, func=AF.Identity, bias=sm[:, i, 4:5],
                                                           scale=sm[:, i, 3:4]),
                               reads=[zb, self.smb3[i], self.smb4[i]], writes=[xhb])
                    else:
                        sch.op("dve", lambda: V.tensor_scalar(out=xh, in0=z, scalar1=sm[:, i, 3:4],
                                                              scalar2=sm[:, i, 4:5], op0=ALU.mult, op1=ALU.add),
                               reads=[zb, self.smb3[i], self.smb4[i]], writes=[xhb])

                def s4():
                    sch.op("dve", lambda: V.tensor_mul(xh, xh, g_ap), reads=[xhb] + gb_bufs, writes=[xhb])
                    sch.op("pool", lambda: G.tensor_add(xh, xh, b_ap), reads=[xhb] + gb_bufs, writes=[xhb])

                return [s1, s2, s3, s4], xh, xhb

            def run(self, i, g_ap, b_ap, gb_bufs):
                sts, xh, xhb = self.stages(i, g_ap, b_ap, gb_bufs)
                for f in sts:
                    f()
                return xh, xhb

        class XTail:
            def __init__(self, es, pfx, bank, nslot=2, cast_eng="act"):
                self.bank = bank
                self.cast_eng = cast_eng
                self.ns = nslot
                self.xbf, self.xbfb = tile(es, pfx + "xbf", [128, nslot, D], BF16, nslot)
                self.xts, self.xtsb = tile(es, pfx + "xts", [128, nslot, 8, 128], BF16, nslot)
                self.k = 0

            def part1(self, xh, xhb, res_ap):
                i = self.k % self.ns
                self.k += 1
                sch.dma("sp", res_ap, xh, reads=[xhb], sembuf=xhb)
                xbf = self.xbf[:, i, :]
                if self.cast_eng == "act":
                    sch.op("act", lambda: A.copy(out=xbf, in_=xh), reads=[xhb], writes=[self.xbfb[i]])
                else:
                    sch.op("pool", lambda: G.tensor_copy(xbf, xh), reads=[xhb], writes=[self.xbfb[i]])
                return i

            def part2(self, i, T_ap, tok0):
                xbf = self.xbf[:, i, :]
                pst = PS[self.bank][:].bitcast(BF16)
                for c in range(8):
                    sch.op("pe", lambda: PE.transpose(pst[:, c * 128:(c + 1) * 128],
                                                      xbf[:, c * 128:(c + 1) * 128], ident[:]),
                           reads=[self.xbfb[i], ident_b], writes=[PB[self.bank]], signal=(c == 7))
                xts = self.xts[:, i, :, :]
                sch.op("dve", lambda: V.tensor_copy(xts, pst[:, 0:1024].rearrange("p (c t) -> p c t", c=8)),
                       reads=[PB[self.bank]], writes=[self.xtsb[i]])
                sch.dma("sp", cpt(T_ap)[:, :, tok0:tok0 + 128], xts, reads=[self.xtsb[i]], sembuf=self.xtsb[i])

            def run(self, xh, xhb, res_ap, T_ap, tok0):
                i = self.part1(xh, xhb, res_ap)
                self.part2(i, T_ap, tok0)

        with ExitStack() as es:
            embt, embt_b = tile(es, "embt", [128, 2, D], F32)
            sch.dma("sp", embt[:], emb_d, writes=[embt_b], sembuf=embt_b)
            ln = LNCtx(es, D, "l0", nslot=8, inplace=True)
            xt = XTail(es, "l0", 7, nslot=4)
            tiles = [(s, t) for s in range(NSEQ) for t in range(NT)]
            groups = [tiles[i:i + 4] for i in range(0, len(tiles), 4)]

            def ld0(g):
                for j, (s, t) in enumerate(groups[g]):
                    i = (g % 2) * 4 + j
                    sch.dma("sp", ln.z[:, i, :], x_d[s, t * 128:(t + 1) * 128, :], writes=[ln.zb[i]],
                            sembuf=ln.zb[i])

            ld0(0)
            for g, grp in enumerate(groups):
                if g + 1 < len(groups):
                    ld0(g + 1)
                chains = []
                for j, (s, t) in enumerate(grp):
                    i = (g % 2) * 4 + j
                    sts, xh, xhb = ln.stages(i, embt[:, 0, :], embt[:, 1, :], [embt_b])
                    hold = {}

                    def p1(xh=xh, xhb=xhb, s=s, t=t, hold=hold):
                        hold["i"] = xt.part1(xh, xhb, xres_d[s, t * 128:(t + 1) * 128, :])

                    def p2(s=s, t=t, hold=hold):
                        xt.part2(hold["i"], xT_d[s], t * 128)

                    chains.append(sts + [p1, p2])
                for k in range(len(chains[0])):
                    for ch in chains:
                        ch[k]()
            sch.barrier()
            if stop == 0:
                raise _Stop()

        for l in range(L):
            lam_init = 0.8 - 0.6 * math.exp(-0.3 * l)
            last = (l == L - 1)
            if True:
                with ExitStack() as les:
                    bincol, bincol_b = tile(les, "bincol", [128, 18], F32)
                    lamv, lamv_b = tile(les, "lamv", [128, 4, 64], F32)
                    lams, lams_b = tile(les, "lams", [128, 2, 64], F32)
                    lamc, lamc_b = tile(les, "lamc", [128, 8], F32)
                    lamc2_b = sch.buf("lamc2")
                    lamc3_b = sch.buf("lamc3")
                    dng, dng_b = tile(les, "dng", [128, 128], F32)
                    sch.dma("sp", bincol[:], bincol_d[:, l, :], writes=[bincol_b], sembuf=bincol_b)
                    sch.dma("sp", lamv[:], lamv_d[:, l, :, :], writes=[lamv_b], sembuf=lamv_b)
                    sch.dma("sp", dng[:], dng_d[:, l, :], writes=[dng_b], sembuf=dng_b)
                    sch.op("dve", lambda: V.tensor_mul(lams[:], lamv[:, 0:2, :], lamv[:, 2:4, :]),
                           reads=[lamv_b], writes=[lams_b])
                    sch.op("dve", lambda: V.reduce_sum(out=lamc[:, 0:2], in_=lams[:], axis=AX.X),
                           reads=[lams_b], writes=[lamc_b])
                    sch.op("act", lambda: A.activation(out=lamc[:, 2:4], in_=lamc[:, 0:2], func=AF.Exp),
                           reads=[lamc_b], writes=[lamc2_b])
                    sch.op("dve", lambda: V.tensor_sub(lamc[:, 4:5], lamc[:, 2:3], lamc[:, 3:4]),
                           reads=[lamc2_b], writes=[lamc3_b])
                    sch.op("dve", lambda: V.tensor_scalar(out=lamc[:, 5:6], in0=lamc[:, 4:5], scalar1=-1.0,
                                                          scalar2=-lam_init, op0=ALU.mult, op1=ALU.add),
                           reads=[lamc3_b], writes=[lamc3_b])
                    neglam = lamc[:, 5:6]
                    sch.op("dve", lambda: V.tensor_scalar_mul(dng[:], dng[:], 1.0 - lam_init), reads=[dng_b],
                           writes=[dng_b])

                    with ExitStack() as es:
                        win, win_b = tile(es, "win", [128, 8, IN_W], BF16)
                        bvt, bvt_b = tile(es, "bvt", [128, 512], F32)
                        sch.dma("sp", bvt[:], bv_d[:, l, :], writes=[bvt_b], sembuf=bvt_b)
                        win_q = [sch.buf(f"winq{q}") for q in range(3)]
                        for q3 in range(3):
                            for c in range(8):
                                sch.dma("pool", win[:, c, q3 * 768:(q3 + 1) * 768],
                                        win_d[l, c * 128:(c + 1) * 128, q3 * 768:(q3 + 1) * 768],
                                        writes=[win_q[q3]], sembuf=win_q[q3])
                        xb, xb_b = tile(es, "xb", [128, 2, 8, 512], BF16, 2)
                        sig, sig_b = tile(es, "sig", [128, 2, 512], F32, 2)
                        ust, ust_b = tile(es, "ust", [128, 2, 2, 512], BF16, 2)
                        qst, qst_b = tile(es, "qst", [128, 2, 4, 512], BF16, 2)
                        kst, kst_b = tile(es, "kst", [128, 2, 4, 512], BF16, 2)
                        qmst, qmst_b = tile(es, "qmst", [128, 2, 2, 512], BF16, 2)
                        vst, vst_b = tile(es, "vst", [128, 2, 4, 512], BF16, 2)
                        cnt = {"ps": 0, "sig": 0}

                        ablocks = [(s_, tb_) for s_ in range(NSEQ) for tb_ in range(NB)]

                        def ldA(bi):
                            s_, tb_ = ablocks[bi]
                            sch.dma("sp", xb[:, bi % 2, :, :], cpt(xT_d[s_])[:, :, tb_ * 512:(tb_ + 1) * 512],
                                    writes=[xb_b[bi % 2]], sembuf=xb_b[bi % 2])

                        ldA(0)
                        for bi, (s, tb) in enumerate(ablocks):
                            if bi + 1 < len(ablocks):
                                ldA(bi + 1)
                            sl = bi % 2
                            t0 = tb * 512

                            def fm_chunk(cc):
                                bk = cnt["ps"] % 4
                                cnt["ps"] += 1
                                for dc in range(8):
                                    sch.op("pe", lambda: PE.matmul(
                                        PS[bk][:], win[:, dc, cc * 128:(cc + 1) * 128], xb[:, sl, dc, :],
                                        start=(dc == 0), stop=(dc == 7)),
                                        reads=[win_q[cc // 6], xb_b[sl]], writes=[PB[bk]], signal=(dc == 7))
                                return bk

                            for c in range(2):
                                bk = fm_chunk(2 + c)
                                si = cnt["sig"] % 2
                                cnt["sig"] += 1
                                sch.op("act", lambda: A.activation(out=sig[:, si, :], in_=PS[bk][:], func=AF.Sigmoid,
                                                                   bias=bincol[:, 2 + c:3 + c], scale=1.0),
                                       reads=[PB[bk], bincol_b], writes=[sig_b[si]])
                                bk2 = fm_chunk(c)
                                sch.op("dve", lambda: V.scalar_tensor_tensor(
                                    out=ust[:, sl, c, :], in0=PS[bk2][:], scalar=bincol[:, c:c + 1], in1=sig[:, si, :],
                                    op0=ALU.add, op1=ALU.mult),
                                    reads=[PB[bk2], bincol_b, sig_b[si]], writes=[ust_b[sl]])
                            sch.dma("sp", cpt(uT_d[s])[:, :, 15 + t0:15 + t0 + 512], ust[:, sl, :, :],
                                    reads=[ust_b[sl]], sembuf=ust_b[sl])
                            for (dst, dstb, cc0, n, dram) in ((qst, qst_b, 4, 4, QT_d), (kst, kst_b, 8, 4, KT_d),
                                                               (qmst, qmst_b, 16, 2, QmT_d)):
                                for h in range(n):
                                    cc = cc0 + h
                                    bk = fm_chunk(cc)
                                    if h % 2 == 0:
                                        sch.op("act", lambda: A.activation(
                                            out=dst[:, sl, h, :], in_=PS[bk][:], func=AF.Identity,
                                            bias=bincol[:, cc:cc + 1], scale=1.0),
                                            reads=[PB[bk], bincol_b], writes=[dstb[sl]])
                                    else:
                                        sch.op("dve", lambda: V.tensor_scalar_add(
                                            dst[:, sl, h, :], PS[bk][:], bincol[:, cc:cc + 1]),
                                            reads=[PB[bk], bincol_b], writes=[dstb[sl]])
                                sch.dma("sp", cpt(dram[s])[:, :, t0:t0 + 512], dst[:, sl, :, :],
                                        reads=[dstb[sl]], sembuf=dstb[sl])
                            for tt in range(4):
                                bk = 4 + tt
                                for dc in range(8):
                                    sch.op("pe", lambda: PE.matmul(
                                        PS[bk][:], xb[:, sl, dc, tt * 128:(tt + 1) * 128], win[:, dc, 1536:2048],
                                        start=(dc == 0), stop=(dc == 7)),
                                        reads=[win_q[2], xb_b[sl]], writes=[PB[bk]], signal=(dc == 7))
                                sch.op("dve", lambda: V.tensor_add(vst[:, sl, tt, :], PS[bk][:], bvt[:]),
                                       reads=[PB[bk], bvt_b], writes=[vst_b[sl]])
                            sch.dma("sp", V_d[s, t0:t0 + 512, :].rearrange("(tt p) n -> p tt n", p=128),
                                    vst[:, sl, :, :], reads=[vst_b[sl]], sembuf=vst_b[sl])
                        sch.barrier()
                        if stop == 1:
                            raise _Stop()

                    for s in range(NSEQ):
                        with ExitStack() as bes:
                            kmT, kmT_b = tile(bes, "kmT", [128, 2, 256], BF16)
                            vma, vma_b = tile(bes, "vma", [128, 2, 4, 65], BF16)
                            btab, btab_b = tile(bes, "btab", [128, 4, 1152], F32)
                            cfar, cfar_b = tile(bes, "cfar", [128, 8], F32)
                            sch.dma("sp", btab[:], btab_d, writes=[btab_b], sembuf=btab_b)
                            sch.op("act", lambda: A.activation(out=btab[:], in_=btab[:], func=AF.Exp),
                                   reads=[btab_b], writes=[btab_b])
                            sch.dma("sp", cfar[:], cfar_d, writes=[cfar_b], sembuf=cfar_b)
                            KT, KT_b = tile(bes, "KT", [128, 4, S], BF16)
                            VA, VA_b = tile(bes, "VA", [128, NT, 4, 129], BF16)
                            wo, wo_b = tile(bes, "wo", [128, 8, D], BF16)
                            vec, vec_b = tile(bes, "vecB", [128, 3, D], F32)
                            sch.op("pool", lambda: G.memset(VA[:, :, :, 128:129], 1.0), writes=[VA_b])
                            for h in range(4):
                                sch.dma("sp", KT[:, h, :], KT_d[s, h], writes=[KT_b], sembuf=KT_b)
                            for t8 in range(0, NT, 8):
                                n8 = min(8, NT - t8)
                                for h in range(4):
                                    sch.dma("sp", VA[:, t8:t8 + n8, h, 0:128],
                                            V_d[s, t8 * 128:(t8 + n8) * 128, h * 128:(h + 1) * 128].rearrange(
                                                "(t p) d -> p t d", p=128),
                                            writes=[VA_b], sembuf=VA_b)
                            for c in range(8):
                                sch.dma("pool", wo[:, c, :], wout_d[l, c * 128:(c + 1) * 128, :], writes=[wo_b],
                                        sembuf=wo_b)
                            sch.dma("sp", vec[:], vec_d[:, l, 0:3, :], writes=[vec_b], sembuf=vec_b)
                            with ExitStack() as es:
                                wkv, wkv_b = tile(es, "wkv", [128, 8, 512], BF16)
                                cwcol, cwcol_b = tile(es, "cwcol", [128, 2, CONV_W], F32)
                                cvec, cvec_b = tile(es, "cvec", [128, 3, 256], F32)
                                sch.dma("sp", cwcol[:], cwcol_d[:, l, :, :], writes=[cwcol_b], sembuf=cwcol_b)
                                sch.dma("sp", cvec[:], cvec_d[:, l, :, :], writes=[cvec_b], sembuf=cvec_b)
                                for c2 in range(4):
                                    sch.dma("pool", wkv[:, 2 * c2:2 * c2 + 2, :],
                                            wkv_d[l, c2 * 256:(c2 + 1) * 256, :].rearrange("(c p) n -> p c n", p=128),
                                            writes=[wkv_b], sembuf=wkv_b)
                                memb, memb_b = tile(es, "memb", [128, 2, D], BF16)
                                memT, memT_b = tile(es, "memT", [128, 8, 256], BF16)
                                sch.dma("pool", memb[:], mem_d[s].rearrange("(a p) n -> p a n", p=128), writes=[memb_b],
                                        sembuf=memb_b)
                                for a in range(2):
                                    pst = PS[a][:].bitcast(BF16)
                                    for c in range(8):
                                        sch.op("pe", lambda: PE.transpose(
                                            pst[:, c * 128:(c + 1) * 128], memb[:, a, c * 128:(c + 1) * 128], ident[:]),
                                            reads=[memb_b, ident_b], writes=[PB[a]], signal=(c == 7))
                                    sch.op("dve", lambda: V.tensor_copy(
                                        memT[:, :, a * 128:(a + 1) * 128],
                                        pst[:, 0:1024].rearrange("p (c t) -> p c t", c=8)),
                                        reads=[PB[a]], writes=[memT_b])
                                for c in range(2):
                                    bk = 2 + c
                                    for dc in range(8):
                                        sch.op("pe", lambda: PE.matmul(
                                            PS[bk][:, 0:256], wkv[:, dc, c * 128:(c + 1) * 128], memT[:, dc, :],
                                            start=(dc == 0), stop=(dc == 7)),
                                            reads=[wkv_b, memT_b], writes=[PB[bk]], signal=(dc == 7))
                                    sch.op("act", lambda: A.copy(out=kmT[:, c, :], in_=PS[bk][:, 0:256]),
                                           reads=[PB[bk]], writes=[kmT_b])
                                sch.op("pool", lambda: G.memset(vma[:], 1.0), writes=[vma_b])
                                for a in range(2):
                                    bk = 4 + a
                                    for dc in range(8):
                                        sch.op("pe", lambda: PE.matmul(
                                            PS[bk][:, 0:256], memT[:, dc, a * 128:(a + 1) * 128], wkv[:, dc, 256:512],
                                            start=(dc == 0), stop=(dc == 7)),
                                            reads=[wkv_b, memT_b], writes=[PB[bk]], signal=(dc == 7))
                                    sch.op("dve", lambda: V.tensor_copy(
                                        vma[:, a, :, 0:64], PS[bk][:, 0:256].rearrange("p (h d) -> p h d", h=4)),
                                        reads=[PB[bk]], writes=[vma_b])

                                diag, diag_b = tile(es, "diag", [128, 2, CONV_W, 128], BF16)
                                dgb = sch.buf("dg")
                                for c in range(2):
                                    for j in range(CONV_W):
                                        if j % 2:
                                            sch.op("pool", lambda: G.tensor_scalar_mul(
                                                diag[:, c, j, :], ident_f[:], cwcol[:, c, j:j + 1]),
                                                reads=[identf_b, cwcol_b], writes=[sch.buf("dgx")])
                                        else:
                                            sch.op("dve", lambda: V.tensor_scalar_mul(
                                                diag[:, c, j, :], ident_f[:], cwcol[:, c, j:j + 1]),
                                                reads=[identf_b, cwcol_b], writes=[sch.buf("dgx")])
                                sch.op("dve", lambda: V.memset(scr[:, 0:1], 0.0), writes=[diag_b])
                                sch.op("pool", lambda: G.memset(scr[:, 1:2], 0.0), writes=[diag_b])
                                ub, ub_b = tile(es, "ub", [128, 2, 2, 542], BF16, 2)
                                lnc = LNCtx(es, 256, "lc", nslot=8, inplace=True)
                                ex, ex_b = tile(es, "cex", [128, 4, 256], F32, 4)
                                cst, cst_b = tile(es, "cst", [128, 2, 4, 256], BF16, 2)

                                def ldU(tb):
                                    sch.dma("sp", ub[:, tb % 2, :, :], cpt(uT_d[s])[:, :, tb * 512:tb * 512 + 542],
                                            writes=[ub_b[tb % 2]], sembuf=ub_b[tb % 2])

                                def conv_mm(tb):
                                    sl = tb % 2
                                    for tt in range(4):
                                        bk = 4 + tt
                                        for c in range(2):
                                            for j in range(CONV_W):
                                                sch.op("pe", lambda: PE.matmul(
                                                    PS[bk][:, c * 128:(c + 1) * 128],
                                                    ub[:, sl, c, tt * 128 + j:tt * 128 + j + 128], diag[:, c, j, :],
                                                    start=(c == 0 and j == 0), stop=(j == CONV_W - 1),
                                                    skip_group_check=True),
                                                    reads=[ub_b[sl], diag_b], writes=[PB[bk]],
                                                    signal=(c == 1 and j == CONV_W - 1))
                                        i = sl * 4 + tt
                                        sch.op("dve", lambda: V.tensor_add(lnc.z[:, i, :], PS[bk][:, 0:256],
                                                                           cvec[:, 0, :]),
                                               reads=[PB[bk], cvec_b], writes=[lnc.zb[i]])

                                def conv_chains(tb):
                                    sl = tb % 2
                                    chains = []
                                    for tt in range(4):
                                        i = sl * 4 + tt
                                        sts, xh, xhb = lnc.stages(i, cvec[:, 1, :], cvec[:, 2, :], [cvec_b])

                                        def s5(xh=xh, xhb=xhb, tt=tt):
                                            sch.op("act", lambda: A.activation(out=ex[:, tt, :], in_=xh, func=AF.Exp,
                                                                               scale=-1.0),
                                                   reads=[xhb], writes=[ex_b[tt]])

                                        def s6(tt=tt):
                                            sch.op("pool", lambda: G.tensor_scalar_add(ex[:, tt, :], ex[:, tt, :], 1.0),
                                                   reads=[ex_b[tt]], writes=[ex_b[tt]])

                                        def s7(xh=xh, xhb=xhb, tt=tt):
                                            sch.op("dve", lambda: V.reciprocal(ex[:, tt, :], ex[:, tt, :]),
                                                   reads=[ex_b[tt]], writes=[ex_b[tt]])
                                            sch.op("dve", lambda: V.tensor_mul(cst[:, sl, tt, :], xh, ex[:, tt, :]),
                                                   reads=[ex_b[tt], xhb], writes=[cst_b[sl]])

                                        chains.append(sts + [s5, s6, s7])
                                    for k in range(len(chains[0])):
                                        for ch in chains:
                                            ch[k]()
                                    sch.dma("sp", cv_d[s, tb * 512:(tb + 1) * 512, :].rearrange("(tt p) n -> p tt n",
                                                                                                p=128),
                                            cst[:, sl, :, :], reads=[cst_b[sl]], sembuf=cst_b[sl])

                                ldU(0)
                                if NB > 1:
                                    ldU(1)
                                conv_mm(0)
                                for tb in range(NB):
                                    if tb + 1 < NB:
                                        conv_mm(tb + 1)
                                    if tb + 2 < NB:
                                        ldU(tb + 2)
                                    conv_chains(tb)
                                sch.barrier()
                                if stop == 2:
                                    raise _Stop()

                            with ExitStack() as es:
                                qtb, qtb_b = tile(es, "qtb", [128, 2, 4, 512], BF16, 2)
                                qmb, qmb_b = tile(es, "qmb", [128, 2, 2, 512], BF16, 2)
                                mixed = es.enter_context(nc.sbuf_tensor(uniq("mixed"), [128, 4, D], BF16))
                                mx_b = [[sch.buf(f"mx{tt}{c}") for c in range(8)] for tt in range(4)]
                                PT, PT_b = tile(es, "PT", [128, 6, 512], BF16, 6)
                                tmp, tmp_b = tile(es, "tmpb", [128, 4, 512], F32, 4)
                                ep, ep_b = tile(es, "ep", [128, 4, 8], F32, 4)
                                oacc = es.enter_context(nc.sbuf_tensor(uniq("oacc"), [128, 2, 8, 129], F32))
                                oacc_b = [[sch.buf(f"oacc{i}{b}") for b in range(3)] for i in range(2)]
                                ew, ew_b = tile(es, "ew", [128, 2, 4, 128], F32, 2)
                                ew2, ew2_b = tile(es, "ew2", [128, 2, 4, 128], F32, 2)
                                esm, esm_b = tile(es, "esm", [128, 2, 16], F32, 2)
                                esm2_b = [sch.buf("esm2") for _ in range(2)]
                                esm3_b = [sch.buf("esm3") for _ in range(2)]
                                esn, esn_b = tile(es, "esn", [128, 2, 8], F32, 2)
                                pend = []
                                mixT, mixT_b = tile(es, "mixT", [128, 2, 8, 128], BF16, 2)
                                xin, xin_b = tile(es, "xin", [128, 2, D], F32, 2)
                                ln1 = LNCtx(es, D, "l1", nslot=4, inplace=True, xh_eng="dve")
                                xt1 = XTail(es, "l1", 7, cast_eng="pool")
                                bg = []
                                cnt = {"pt": 0, "mpt": 0, "tmp": 0, "tile": 0, "oa": 0}

                                def ldQ(qb):
                                    sl = qb % 2
                                    sch.dma("sp", qtb[:, sl, :, :], cpt(QT_d[s])[:, :, qb * 512:(qb + 1) * 512],
                                            writes=[qtb_b[sl]], sembuf=qtb_b[sl])
                                    sch.dma("sp", qmb[:, sl, :, :], cpt(QmT_d[s])[:, :, qb * 512:(qb + 1) * 512],
                                            writes=[qmb_b[sl]], sembuf=qmb_b[sl])

                                def ldX(k):
                                    xi = k % 2
                                    sch.dma("sp", xin[:, xi, :], xres_d[s, k * 128:(k + 1) * 128, :], writes=[xin_b[xi]],
                                            sembuf=xin_b[xi])

                                ones_bf, ones_b = tile(es, "ones_bf", [1, 128], BF16)
                                bout_bf, bout_b = tile(es, "bout_bf", [1, D], BF16)
                                sch.op("pool", lambda: G.memset(ones_bf[:], 1.0), writes=[ones_b])
                                sch.op("pool", lambda: G.tensor_copy(bout_bf[:], vec[0:1, 0, :]), reads=[vec_b],
                                       writes=[bout_b])
                                ldQ(0)
                                ldX(0)
                                for qb in range(NB):
                                    sl = qb % 2
                                    q0 = qb * 512
                                    if qb + 1 < NB:
                                        ldQ(qb + 1)
                                    for tt in range(4):
                                        sch.dma("sp", mixed[:, tt, 0:256], cv_d[s, q0 + tt * 128:q0 + (tt + 1) * 128, :],
                                                writes=[mx_b[tt][0], mx_b[tt][1]], sembuf=mx_b[tt][0])
                                    if 'm' in os.environ.get('BPARTS', 'mdo'):
                                        def m_qk(h):
                                            pr = slice((h % 2) * 64, (h % 2) * 64 + 64)
                                            hc = h // 2
                                            b0 = (h % 2) * 2
                                            for mt in range(2):
                                                sch.op("pe", lambda: PE.matmul(
                                                    PS[b0 + mt][:], kmT[pr, hc, mt * 128:(mt + 1) * 128],
                                                    qmb[pr, sl, hc, :], start=True, stop=True),
                                                    reads=[kmT_b, qmb_b[sl]], writes=[PB[b0 + mt]])

                                        def m_ex(h):
                                            b0 = (h % 2) * 2
                                            pi = (cnt["mpt"] % 3) * 2
                                            cnt["mpt"] += 1
                                            sch.op("act", lambda: A.activation(
                                                out=PT[:, pi:pi + 2, :].rearrange("p a b -> p (a b)"),
                                                in_=psall[:, b0 * 512:(b0 + 2) * 512], func=AF.Exp, scale=0.125),
                                                reads=[PB[b0], PB[b0 + 1]], writes=[PT_b[pi], PT_b[pi + 1]])
                                            return [pi, pi + 1]

                                        def m_pv(h, pts):
                                            ab = 4 + (h % 2)
                                            for qt in range(4):
                                                for mt in range(2):
                                                    sch.op("pe", lambda: PE.matmul(
                                                        PS[ab][:, qt * 65:qt * 65 + 65],
                                                        PT[:, pts[mt], qt * 128:(qt + 1) * 128], vma[:, mt, h, :],
                                                        start=(qt == 0 and mt == 0), stop=(mt == 1),
                                                        skip_group_check=True),
                                                        reads=[PT_b[pts[mt]], vma_b], writes=[PB[ab]],
                                                        signal=(qt == 3 and mt == 1))
                                            ei = h
                                            sch.op("dve", lambda: V.reciprocal(
                                                ep[:, ei, 0:4],
                                                PS[ab][:, 0:260].rearrange("p (q e) -> p q e", e=65)[:, :, 64]),
                                                reads=[PB[ab]], writes=[ep_b[ei]])
                                            sch.op("dve", lambda: V.tensor_mul(
                                                mixed[:, :, 768 + h * 64:768 + (h + 1) * 64],
                                                PS[ab][:, 0:260].rearrange("p (q e) -> p q e", e=65)[:, :, 0:64],
                                                ep[:, ei, 0:4].unsqueeze(2).to_broadcast([128, 4, 64])),
                                                reads=[PB[ab], ep_b[ei]], writes=[mx_b[qt][6 + h // 2] for qt in range(4)])

                                        m_qk(0)
                                        for h in range(4):
                                            if h + 1 < 4:
                                                m_qk(h + 1)
                                            pts = m_ex(h)
                                            m_pv(h, pts)
                                    steps = [(h, kt) for h in range(4) for kt in range(NT)]
                                    sched_epi = {}

                                    def qk(i):
                                        h, kt = steps[i]
                                        for m in range(2):
                                            bk = (kt % 2) * 2 + m
                                            pr = slice(m * 64, m * 64 + 64)
                                            sch.op("pe", lambda: PE.matmul(
                                                PS[bk][:], KT[pr, h, kt * 128:(kt + 1) * 128], qtb[pr, sl, h, :],
                                                start=True, stop=True),
                                                reads=[KT_b, qtb_b[sl]], writes=[PB[bk]])

                                    def ex(i):
                                        h, kt = steps[i]
                                        off = kt * 128 - q0
                                        near = (-256 < off < 640)
                                        b0 = (kt % 2) * 2
                                        pi = (cnt["pt"] % 3) * 2
                                        cnt["pt"] += 1
                                        pis = [pi, pi + 1]
                                        pt2 = PT[:, pi:pi + 2, :].rearrange("p a b -> p (a b)")
                                        if near:
                                            ts_ = (cnt["tmp"] % 2) * 2
                                            cnt["tmp"] += 1
                                            sch.op("act", lambda: A.activation(
                                                out=tmp[:, ts_:ts_ + 2, :].rearrange("p a b -> p (a b)"),
                                                in_=psall[:, b0 * 512:(b0 + 2) * 512], func=AF.Exp, scale=0.125),
                                                reads=[PB[b0], PB[b0 + 1]], writes=[tmp_b[ts_], tmp_b[ts_ + 1]])
                                            sch.op("dve", lambda: V.tensor_mul(
                                                PT[:, pi:pi + 2, :], tmp[:, ts_:ts_ + 2, :],
                                                btab[:, h, 512 - off:1024 - off].unsqueeze(1).to_broadcast(
                                                    [128, 2, 512])),
                                                reads=[tmp_b[ts_], tmp_b[ts_ + 1], btab_b],
                                                writes=[PT_b[pi], PT_b[pi + 1]])
                                        else:
                                            ci = h * 2 + (1 if off > 0 else 0)
                                            sch.op("act", lambda: A.activation(
                                                out=pt2, in_=psall[:, b0 * 512:(b0 + 2) * 512], func=AF.Exp,
                                                bias=cfar[:, ci:ci + 1], scale=0.125),
                                                reads=[PB[b0], PB[b0 + 1], cfar_b], writes=[PT_b[pi], PT_b[pi + 1]])
                                        return pis

                                    def pv(i, pis):
                                        h, kt = steps[i]
                                        for m in range(2):
                                            for qt in range(4):
                                                idx = m * 4 + qt
                                                bk = 4 + idx // 3
                                                col = (idx % 3) * 129
                                                sch.op("pe", lambda: PE.matmul(
                                                    PS[bk][:, col:col + 129],
                                                    PT[:, pis[m], qt * 128:(qt + 1) * 128], VA[:, kt, h, :],
                                                    start=(kt == 0 and (idx % 3) == 0), stop=(kt == NT - 1),
                                                    skip_group_check=True),
                                                    reads=[PT_b[pis[m]], VA_b], writes=[PB[bk]],
                                                    signal=(idx % 3 == 2 or idx == 7))

                                    def evac(h):
                                        oi = cnt["oa"] % 2
                                        cnt["oa"] += 1
                                        for b in range(3):
                                            n = 3 if b < 2 else 2
                                            sch.op("dve", lambda: V.tensor_copy(
                                                oacc[:, oi, b * 3:b * 3 + n, :],
                                                PS[4 + b][:, 0:n * 129].rearrange("p (a c) -> p a c", c=129)),
                                                reads=[PB[4 + b]], writes=[oacc_b[oi][b]])
                                        return oi

                                    def epi1(h, oi):
                                        en = "dve" if h == 3 else "pool"
                                        E_ = V if h == 3 else G
                                        sch.op("dve", lambda: V.reciprocal(esm[:, oi, 0:8], oacc[:, oi, :, 128]),
                                               reads=oacc_b[oi], writes=[esm_b[oi]])
                                        sch.op(en, lambda: E_.tensor_scalar_mul(esm[:, oi, 8:12], esm[:, oi, 4:8],
                                                                                   neglam),
                                               reads=[esm_b[oi], lamc3_b], writes=[esm2_b[oi]])
                                        sch.op(en, lambda: E_.tensor_mul(
                                            ew[:, oi, :, :], oacc[:, oi, 0:4, 0:128],
                                            esm[:, oi, 0:4].unsqueeze(2).to_broadcast([128, 4, 128])),
                                            reads=oacc_b[oi] + [esm_b[oi]], writes=[ew_b[oi]])
                                        sch.op(en, lambda: E_.tensor_mul(
                                            ew2[:, oi, :, :], oacc[:, oi, 4:8, 0:128],
                                            esm[:, oi, 8:12].unsqueeze(2).to_broadcast([128, 4, 128])),
                                            reads=oacc_b[oi] + [esm2_b[oi]], writes=[ew2_b[oi]])
                                        sch.op(en, lambda: E_.tensor_add(ew[:, oi, :, :], ew[:, oi, :, :],
                                                                            ew2[:, oi, :, :]),
                                               reads=[ew_b[oi], ew2_b[oi]], writes=[ew_b[oi]])
                                        sch.op(en, lambda: E_.tensor_mul(ew2[:, oi, :, :], ew[:, oi, :, :],
                                                                            ew[:, oi, :, :]),
                                               reads=[ew_b[oi]], writes=[ew2_b[oi]])

                                    def epi1b(h, oi):
                                        sch.op("dve", lambda: V.reduce_sum(out=esm[:, oi, 12:16], in_=ew2[:, oi, :, :],
                                                                           axis=AX.X),
                                               reads=[ew2_b[oi]], writes=[esm3_b[oi]])

                                    def epi2(h, oi):
                                        sch.op("act", lambda: A.activation(
                                            out=esn[:, oi, 0:4], in_=esm[:, oi, 12:16], func=AF.Ln, bias=epsc[:],
                                            scale=1.0 / 128.0),
                                            reads=[esm3_b[oi], epsc_b], writes=[esn_b[oi]])
                                        sch.op("act", lambda: A.activation(
                                            out=esn[:, oi, 4:8], in_=esn[:, oi, 0:4], func=AF.Exp, scale=-0.5),
                                            reads=[esn_b[oi]], writes=[esn_b[oi]])

                                    def epi3(h, oi):
                                        en = "dve" if h == 3 else "pool"
                                        E_ = V if h == 3 else G
                                        sch.op(en, lambda: E_.tensor_mul(
                                            ew[:, oi, :, :], ew[:, oi, :, :],
                                            esn[:, oi, 4:8].unsqueeze(2).to_broadcast([128, 4, 128])),
                                            reads=[ew_b[oi], esn_b[oi]], writes=[ew_b[oi]])
                                        sch.op(en, lambda: E_.tensor_mul(
                                            mixed[:, :, 256 + h * 128:256 + (h + 1) * 128], ew[:, oi, :, :],
                                            dng[:].unsqueeze(1).to_broadcast([128, 4, 128])),
                                            reads=[ew_b[oi], dng_b], writes=[mx_b[qt][2 + h] for qt in range(4)])

                                    def run_sched(i):
                                        for fn in sched_epi.pop(i, []):
                                            fn()

                                    if 'd' in os.environ.get('BPARTS', 'mdo'):
                                        qk(0)
                                        for i, (h, kt) in enumerate(steps):
                                            if i + 1 < len(steps):
                                                qk(i + 1)
                                            pis = ex(i)
                                            pv(i, pis)
                                            if i >= 2 and (i % 2 == 0) and bg:
                                                bg.pop(0)()
                                            if kt == NT - 1:
                                                for j in sorted(sched_epi):
                                                    for fn in sched_epi[j]:
                                                        fn()
                                                sched_epi.clear()
                                                oi = evac(h)
                                                sched_epi.setdefault(i + 2, []).append(lambda h=h, oi=oi: epi1(h, oi))
                                                sched_epi.setdefault(i + 10, []).append(lambda h=h, oi=oi: epi1b(h, oi))
                                                sched_epi.setdefault(i + 13, []).append(lambda h=h, oi=oi: epi2(h, oi))
                                                sched_epi.setdefault(i + 16, []).append(lambda h=h, oi=oi: epi3(h, oi))
                                            run_sched(i)
                                        for i in sorted(sched_epi):
                                            for fn in sched_epi[i]:
                                                fn()
                                        sched_epi.clear()
                                    for fn in pend:
                                        fn()
                                    pend.clear()
                                    pst = PS[7][:].bitcast(BF16)

                                    def mixT_of(tt, k):
                                        mi = k % 2
                                        for c in range(8):
                                            sch.op("pe", lambda: PE.transpose(
                                                pst[:, c * 128:(c + 1) * 128], mixed[:, tt, c * 128:(c + 1) * 128],
                                                ident[:]),
                                                reads=[mx_b[tt][c], ident_b], writes=[PB[7]], signal=(c == 7))
                                        sch.op("dve", lambda: V.tensor_copy(
                                            out=mixT[:, mi, :, :], in_=pst[:, 0:1024].rearrange("p (c t) -> p c t", c=8)),
                                            reads=[PB[7]], writes=[mixT_b[mi]])

                                    if 'o' in os.environ.get('BPARTS', 'mdo'):
                                        for fn in bg:
                                            fn()
                                        bg.clear()
                                        mixT_of(0, cnt["tile"])
                                        chains = []
                                        for tt in range(4):
                                            k = cnt["tile"]
                                            cnt["tile"] += 1
                                            mi = k % 2
                                            xi = k % 2
                                            tok0 = q0 + tt * 128
                                            if k + 1 < NT:
                                                ldX(k + 1)
                                            if tt + 1 < 4:
                                                mixT_of(tt + 1, k + 1)
                                            zi = ln1.slot()
                                            for hf in range(2):
                                                bk = (tt % 3) * 2 + hf
                                                sch.op("pe", lambda: PE.matmul(
                                                    PS[bk][:], ones_bf[:], bout_bf[:, hf * 512:(hf + 1) * 512],
                                                    start=True, stop=False),
                                                    reads=[ones_b, bout_b], writes=[PB[bk]], signal=False)
                                                for c in range(8):
                                                    sch.op("pe", lambda: PE.matmul(
                                                        PS[bk][:], mixT[:, mi, c, :], wo[:, c, hf * 512:(hf + 1) * 512],
                                                        start=False, stop=(c == 7)),
                                                        reads=[mixT_b[mi], wo_b], writes=[PB[bk]], signal=(c == 7))
                                                sch.op("dve", lambda: V.scalar_tensor_tensor(
                                                    out=ln1.z[:, zi, hf * 512:(hf + 1) * 512],
                                                    in0=xin[:, xi, hf * 512:(hf + 1) * 512], scalar=ALPHA, in1=PS[bk][:],
                                                    op0=ALU.mult, op1=ALU.add),
                                                    reads=[PB[bk], xin_b[xi]], writes=[ln1.zb[zi]])
                                            sts, xh, xhb = ln1.stages(zi, vec[:, 1, :], vec[:, 2, :], [vec_b])
                                            hold = {}

                                            def p1(xh=xh, xhb=xhb, tok0=tok0, hold=hold):
                                                hold["i"] = xt1.part1(xh, xhb, x1res_d[s, tok0:tok0 + 128, :])

                                            def p2(tok0=tok0, hold=hold):
                                                xt1.part2(hold["i"], x1T_d[s], tok0)

                                            chains.append(sts + [p1, p2])
                                        for a, b in ((0, 1), (2, 3)):
                                            for fa, fb in zip(chains[a], chains[b]):
                                                bg.append(fa)
                                                bg.append(fb)
                                for fn in bg:
                                    fn()
                                bg.clear()
                                for fn in pend:
                                    fn()
                                pend.clear()
                                sch.barrier()
                                if stop == 3:
                                    raise _Stop()

            with ExitStack() as es:
                wup, wup_b = tile(es, "wup", [128, 8, D_FF], BF16)
                wdn, wdn_b = tile(es, "wdn", [128, 32, D], BF16)
                vec, vec_b = tile(es, "vecE", [128, 2, D], F32)
                wup_q = [sch.buf(f"wupq{q}") for q in range(4)]
                wdn_q = [sch.buf(f"wdnq{q}") for q in range(4)]
                for q4 in range(4):
                    for c in range(8):
                        sch.dma("pool", wup[:, c, q4 * 1024:(q4 + 1) * 1024],
                                wup_d[l, c * 128:(c + 1) * 128, q4 * 1024:(q4 + 1) * 1024], writes=[wup_q[q4]],
                                sembuf=wup_q[q4])
                for a in range(32):
                    sch.dma("pool", wdn[:, a, :], wdn_d[l, a * 128:(a + 1) * 128, :], writes=[wdn_q[a // 8]],
                            sembuf=wdn_q[a // 8])
                sch.dma("sp", vec[:], vec_d[:, l, 3:5, :], writes=[vec_b], sembuf=vec_b)
                xb, xb_b = tile(es, "xbE", [128, 8, 512], BF16)
                hT = es.enter_context(nc.sbuf_tensor(uniq("hT"), [128, 32, 512], BF16))
                hT_b = [sch.buf(f"hT{f}") for f in range(32)]
                rl, rl_b = tile(es, "rl", [128, 2, 512], F32, 2)
                xin, xin_b = tile(es, "xinE", [128, 1, D], F32, 1)
                ln2 = LNCtx(es, D, "l2", nslot=1)
                xt2 = XTail(es, "l2", 3, nslot=1)
                cnt = {"up": 0}
                pend = []
                blocks = [(s, tb) for s in range(NSEQ) for tb in range(NB)]

                def ldE(bi):
                    s_, tb_ = blocks[bi]
                    sch.dma("sp", xb[:], cpt(x1T_d[s_])[:, :, tb_ * 512:(tb_ + 1) * 512], writes=[xb_b], sembuf=xb_b)

                ldE(0)
                for bi, (s, tb) in enumerate(blocks):
                    t0 = tb * 512
                    for f in range(32):
                        bk = cnt["up"] % 3
                        ri = cnt["up"] % 2
                        cnt["up"] += 1
                        for dc in range(8):
                            sch.op("pe", lambda: PE.matmul(
                                PS[bk][:], wup[:, dc, f * 128:(f + 1) * 128], xb[:, dc, :],
                                start=(dc == 0), stop=(dc == 7)),
                                reads=[wup_q[f // 8], xb_b], writes=[PB[bk]], signal=(dc == 7))
                        sch.op("act", lambda: A.activation(out=rl[:, ri, :], in_=PS[bk][:], func=AF.Relu),
                               reads=[PB[bk]], writes=[rl_b[ri]])
                        if f % 4:
                            sch.op("dve", lambda: V.tensor_mul(hT[:, f, :], rl[:, ri, :], rl[:, ri, :]),
                                   reads=[rl_b[ri]], writes=[hT_b[f]])
                        else:
                            sch.op("pool", lambda: G.tensor_mul(hT[:, f, :], rl[:, ri, :], rl[:, ri, :]),
                                   reads=[rl_b[ri]], writes=[hT_b[f]])
                        if f == 6:
                            for fn in pend:
                                fn()
                            pend.clear()
                    if bi + 1 < len(blocks):
                        ldE(bi + 1)
                    for tt in range(4):
                        tok0 = t0 + tt * 128
                        sch.dma("sp", xin[:, 0, :], x1res_d[s, tok0:tok0 + 128, :], writes=[xin_b[0]],
                                sembuf=xin_b[0])
                        zi = ln2.slot()
                        for hf in range(2):
                            bk = 4 + (tt % 2) * 2 + hf
                            for f in range(32):
                                sch.op("pe", lambda: PE.matmul(
                                    PS[bk][:], hT[:, f, tt * 128:(tt + 1) * 128],
                                    wdn[:, f, hf * 512:(hf + 1) * 512], start=(f == 0), stop=(f == 31)),
                                    reads=[hT_b[f], wdn_q[f // 8]], writes=[PB[bk]], signal=(f == 31))
                            sch.op("dve", lambda: V.scalar_tensor_tensor(
                                out=ln2.z[:, zi, hf * 512:(hf + 1) * 512],
                                in0=xin[:, 0, hf * 512:(hf + 1) * 512], scalar=ALPHA, in1=PS[bk][:],
                                op0=ALU.mult, op1=ALU.add),
                                reads=[PB[bk], xin_b[0]], writes=[ln2.zb[zi]])
                        for fn in pend:
                            fn()
                        pend.clear()
                        xh, xhb = ln2.run(zi, vec[:, 0, :], vec[:, 1, :], [vec_b])
                        dst = out_d if last else xres_d
                        xi2 = xt2.part1(xh, xhb, dst[s, tok0:tok0 + 128, :])
                        if not last:
                            pend.append(lambda xi2=xi2, s=s, tok0=tok0: xt2.part2(xi2, xT_d[s], tok0))
                for fn in pend:
                    fn()
                pend.clear()
                sch.barrier()
                if stop == 4:
                    raise _Stop()
    return nc


def host_inputs(S, L, NSEQ, inp, core):
    f = np.float32
    b0 = core * NSEQ
    rel = (np.arange(128)[:, None] - np.arange(1152)[None, :] + 512)
    bk = t5_bucket_np(rel)
    rb = np.asarray(inp["rel_bias"], f)
    btab = np.ascontiguousarray(rb[bk].transpose(0, 2, 1))
    cfar = np.empty((128, 8), f)
    for h in range(4):
        cfar[:, 2 * h] = rb[15, h]
        cfar[:, 2 * h + 1] = rb[31, h]
    rep = lambda a: np.ascontiguousarray(np.broadcast_to(np.asarray(a, f)[None], (128,) + tuple(np.shape(a))))
    b_in = np.asarray(inp["b_in"], f)[:L]
    m = {
        "x": np.ascontiguousarray(np.asarray(inp["x"], f)[b0:b0 + NSEQ]),
        "mem": np.ascontiguousarray(np.asarray(inp["mem"], f)[b0:b0 + NSEQ]),
        "emb": rep(np.stack([inp["emb_ln_g"], inp["emb_ln_b"]])),
        "btab": btab, "cfar": cfar,
        "w_in": np.ascontiguousarray(np.asarray(inp["w_in"], f)[:L]),
        "w_mem_kv": np.ascontiguousarray(np.asarray(inp["w_mem_kv"], f)[:L]),
        "w_out": np.ascontiguousarray(np.asarray(inp["w_out"], f)[:L]),
        "w_up": np.ascontiguousarray(np.asarray(inp["w_up"], f)[:L]),
        "w_down": np.ascontiguousarray(np.asarray(inp["w_down"], f)[:L]),
        "bincol": np.ascontiguousarray(b_in.reshape(L, 18, 128).transpose(2, 0, 1)),
        "bvt": rep(b_in[:, 1536:2048]),
        "cwcol": np.ascontiguousarray(np.asarray(inp["conv_w"], f)[:L].reshape(L, CONV_W, 2, 128).transpose(3, 0, 2, 1)),
        "cvec": rep(np.stack([np.asarray(inp["conv_b"], f)[:L], np.asarray(inp["conv_ln_g"], f)[:L],
                              np.asarray(inp["conv_ln_b"], f)[:L]], axis=1)),
        "lamv": rep(np.stack([np.asarray(inp["lambda_q1"], f)[:L], np.asarray(inp["lambda_q2"], f)[:L],
                              np.asarray(inp["lambda_k1"], f)[:L], np.asarray(inp["lambda_k2"], f)[:L]], axis=1)),
        "dng": rep(np.asarray(inp["diff_norm_g"], f)[:L]),
        "vec": rep(np.stack([np.asarray(inp["b_out"], f)[:L], np.asarray(inp["ln1_g"], f)[:L],
                             np.asarray(inp["ln1_b"], f)[:L], np.asarray(inp["ln2_g"], f)[:L],
                             np.asarray(inp["ln2_b"], f)[:L]], axis=1)),
    }
    return m


_NC_CACHE = {}


def kernel(**inputs):
    x = np.asarray(inputs["x"])
    B, S, _ = x.shape
    L = DEPTH
    NSEQ = B // NCORES
    key = (S, L, NSEQ)
    if key not in _NC_CACHE:
        _NC_CACHE[key] = build(S, L, NSEQ)
    nc = _NC_CACHE[key]
    in_maps = [host_inputs(S, L, NSEQ, inputs, c) for c in range(NCORES)]
    res = run_bass_kernel_spmd(nc, in_maps, core_ids=list(range(NCORES)))
    out = np.concatenate([np.asarray(r["out"]) for r in res.results], axis=0)
    return out.astype(np.float32)
```

```python
import math
import os
from contextlib import ExitStack

import numpy as np
import concourse.bass as bass
import concourse.mybir as mybir
from concourse.bass_utils import run_bass_kernel_spmd

F32 = mybir.dt.float32
BF16 = mybir.dt.bfloat16
AF = mybir.ActivationFunctionType
ALU = mybir.AluOpType
AX = mybir.AxisListType

D = 1024
DEPTH = 4
MEM_LEN = 256
CONV_W = 31
IN_W = 2304
D_FF = 4096
ALPHA = (2.0 * DEPTH) ** 0.25
LN_EPS = 1e-5
NCORES = 8


class Buf:
    __slots__ = ("name", "w", "r", "excl", "sem")

    def __init__(self, name, excl=False):
        self.name = name
        self.w = None
        self.r = {}
        self.excl = excl
        self.sem = None


class Sched:
    def __init__(self, nc, es):
        self.nc = nc
        self.h = {"pe": nc.tensor, "act": nc.scalar, "dve": nc.vector, "pool": nc.gpsimd, "sp": nc.sync}
        self.sem = {e: es.enter_context(nc.semaphore("s_" + e)) for e in ("pe", "act", "dve", "pool")}
        self.cnt = {e: 0 for e in self.sem}
        self.seen = {e: {} for e in self.h}
        self.pending = {e: [] for e in self.sem}
        self.bar = es.enter_context(nc.semaphore("s_bar"))
        self.nbar = 0
        self.dsems = [es.enter_context(nc.semaphore(f"s_d{i}")) for i in range(90)]
        self.dval = [0] * len(self.dsems)
        self.dfree = {"pool": list(range(16)), "sp": list(range(16, len(self.dsems)))}
        self.dq = {}
        self.dbufs = []
        self.allbufs = []
        self.pool_out = []

    def buf(self, name, excl=False):
        b = Buf(name, excl)
        self.allbufs.append(b)
        return b

    def _semof(self, key):
        return self.sem[key] if isinstance(key, str) else self.dsems[key[1]]

    def _deps(self, eng, reads, writes):
        deps = {}

        def add(tok):
            if tok is None:
                return
            k, v = tok
            if deps.get(k, 0) < v:
                deps[k] = v

        for b in reads:
            add(b.w)
            if b.excl:
                for k, v in b.r.items():
                    if k != eng:
                        add((k, v))
        for b in writes:
            add(b.w)
            for k, v in b.r.items():
                add((k, v))
        return deps

    def _wait(self, eng, deps):
        seen = self.seen[eng]
        for k, v in deps.items():
            if k == eng and eng == "pe":
                continue
            if seen.get(k, 0) >= v:
                continue
            self.h[eng].wait_ge(self._semof(k), v)
            seen[k] = v

    def _mark(self, tok, reads, writes):
        k, v = tok
        for b in reads:
            if b.r.get(k, 0) < v:
                b.r[k] = v
        for b in writes:
            b.w = tok
            b.r = {}

    def op(self, eng, fn, reads=(), writes=(), signal=True):
        deps = self._deps(eng, reads, writes)
        self._wait(eng, deps)
        ins = fn()
        if not signal:
            self.pending[eng].append((list(reads), list(writes)))
            return None
        ins.then_inc(self.sem[eng], 1)
        self.cnt[eng] += 1
        tok = (eng, self.cnt[eng])
        for r_, w_ in self.pending[eng]:
            self._mark(tok, r_, w_)
        self.pending[eng] = []
        self._mark(tok, reads, writes)
        return tok

    def dma(self, q, out, in_, reads=(), writes=(), sembuf=None):
        deps = self._deps(q, reads, writes)
        self._wait(q, deps)
        if sembuf.sem is None:
            assert self.dfree[q], "out of dma semaphores"
            sembuf.sem = self.dfree[q].pop(0)
            self.dq[sembuf.sem] = q
            self.dbufs.append(sembuf)
        assert self.dq[sembuf.sem] == q, "buffer DMA'd from two queues"
        ins = self.h[q].dma_start(out=out, in_=in_)
        ins.then_inc(self.dsems[sembuf.sem], 16)
        self.dval[sembuf.sem] += 16
        tok = (("d", sembuf.sem), self.dval[sembuf.sem])
        self._mark(tok, reads, writes)
        if q == "pool":
            self.pool_out.append(tok)
            if len(self.pool_out) > 3:
                k, v = self.pool_out.pop(0)
                self._wait("pool", {k: v})
        return tok

    def barrier(self):
        for e in self.pending:
            assert not self.pending[e], f"pending ops on {e} at barrier"
        sp = self.h["sp"]
        seen = self.seen["sp"]
        for e in self.sem:
            if seen.get(e, 0) < self.cnt[e]:
                sp.wait_ge(self.sem[e], self.cnt[e])
                seen[e] = self.cnt[e]
        for b in self.dbufs:
            k = ("d", b.sem)
            if seen.get(k, 0) < self.dval[b.sem]:
                sp.wait_ge(self.dsems[b.sem], self.dval[b.sem])
                seen[k] = self.dval[b.sem]
        sp.sem_inc(self.bar, 1)
        self.nbar += 1
        for e in self.sem:
            self.h[e].wait_ge(self.bar, self.nbar)
        for e in self.h:
            for k2 in self.sem:
                self.seen[e][k2] = self.cnt[k2]
            for i in range(len(self.dsems)):
                self.seen[e][("d", i)] = self.dval[i]
        for b in self.dbufs:
            self.dfree[self.dq[b.sem]].append(b.sem)
            b.sem = None
        self.dbufs = []
        self.pool_out = []
        for b in self.allbufs:
            b.w = None
            b.r = {}
        self.allbufs = [b for b in self.allbufs if b.excl]


def t5_bucket_np(rel):
    rel = np.asarray(rel, np.int64)
    half, max_exact = 16, 8
    ret = (rel > 0).astype(np.int64) * half
    n = np.abs(rel)
    nf = np.maximum(n, 1).astype(np.float32)
    large = max_exact + (np.log(nf / np.float32(max_exact)) / np.float32(math.log(128 / max_exact))
                         * np.float32(half - max_exact)).astype(np.int32)
    large = np.minimum(large, half - 1)
    return ret + np.where(n < max_exact, n, large)


class _Stop(Exception):
    pass


def build(S, L, NSEQ, dbg=False, stop=99):
    nc = bass.Bass("TRN2", target_bir_lowering=False)
    try:
        _build(nc, S, L, NSEQ, dbg, stop)
    except _Stop:
        pass
    return nc


def _build(nc, S, L, NSEQ, dbg, stop):
    NT = S // 128
    NB = S // 512
    SP = S + 30

    def din(name, shape, dt=F32):
        return nc.dram_tensor(name, shape, dt, kind="ExternalInput").ap()

    x_d = din("x", [NSEQ, S, D])
    mem_d = din("mem", [NSEQ, MEM_LEN, D])
    emb_d = din("emb", [128, 2, D])
    btab_d = din("btab", [128, 4, 1152])
    cfar_d = din("cfar", [128, 8])
    win_d = din("w_in", [L, D, IN_W])
    wkv_d = din("w_mem_kv", [L, D, 512])
    wout_d = din("w_out", [L, D, D])
    wup_d = din("w_up", [L, D, D_FF])
    wdn_d = din("w_down", [L, D_FF, D])
    bincol_d = din("bincol", [128, L, 18])
    bv_d = din("bvt", [128, L, 512])
    cwcol_d = din("cwcol", [128, L, 2, CONV_W])
    cvec_d = din("cvec", [128, L, 3, 256])
    lamv_d = din("lamv", [128, L, 4, 64])
    dng_d = din("dng", [128, L, 128])
    vec_d = din("vec", [128, L, 5, D])
    out_d = nc.dram_tensor("out", [NSEQ, S, D], F32, kind="ExternalOutput").ap()

    sk = "ExternalOutput" if dbg else "Internal"

    def dsc(name, shape, dt):
        return nc.dram_tensor(name, shape, dt, kind=sk).ap()

    xres_d = dsc("xres", [NSEQ, S, D], F32)
    x1res_d = dsc("x1res", [NSEQ, S, D], F32)
    xT_d = dsc("xT", [NSEQ, 8, 128, S], BF16)
    x1T_d = dsc("x1T", [NSEQ, 8, 128, S], BF16)
    QT_d = dsc("QT", [NSEQ, 4, 128, S], BF16)
    KT_d = dsc("KT", [NSEQ, 4, 128, S], BF16)
    V_d = dsc("V", [NSEQ, S, 512], BF16)
    QmT_d = dsc("QmT", [NSEQ, 2, 128, S], BF16)
    uT_d = dsc("uT", [NSEQ, 2, 128, SP], BF16)
    cv_d = dsc("cv", [NSEQ, S, 256], BF16)

    def cpt(ap):
        return ap.rearrange("c p t -> p c t")

    with ExitStack() as ges:
        sch = Sched(nc, ges)
        V, G, A, PE = nc.vector, nc.gpsimd, nc.scalar, nc.tensor

        uq = [0]

        def uniq(name):
            uq[0] += 1
            return f"{name}_{uq[0]}"

        def tile(es, name, shape, dt, nb=0, excl=False):
            t = es.enter_context(nc.sbuf_tensor(uniq(name), shape, dt))
            if nb == 0:
                return t, sch.buf(name, excl)
            return t, [sch.buf(f"{name}.{i}", excl) for i in range(nb)]

        psall = ges.enter_context(nc.psum_tensor("psall", [128, 4096], F32))
        PS = [psall[:, i * 512:(i + 1) * 512] for i in range(8)]
        PB = [sch.buf(f"ps{i}", excl=True) for i in range(8)]

        ident_f, identf_b = tile(ges, "ident_f", [128, 128], F32)
        ident, ident_b = tile(ges, "ident", [128, 128], BF16)
        epsc, epsc_b = tile(ges, "epsc", [128, 1], F32)
        zero_bf, zero_b = tile(ges, "zero_bf", [128, 32], BF16)
        scr, scr_b = tile(ges, "scr", [128, 8], F32)

        sch.op("pool", lambda: G.memset(ident_f[:], 1.0), writes=[identf_b])
        sch.op("pool", lambda: G.affine_select(out=ident_f[:], in_=ident_f[:], pattern=[[-1, 128]],
                                               compare_op=ALU.is_equal, fill=0.0, base=0, channel_multiplier=1),
               reads=[identf_b], writes=[identf_b])
        sch.op("dve", lambda: V.tensor_copy(ident[:], ident_f[:]), reads=[identf_b], writes=[ident_b])
        sch.op("dve", lambda: V.memset(epsc[:], LN_EPS), writes=[epsc_b])
        sch.op("dve", lambda: V.memset(zero_bf[:], 0.0), writes=[zero_b])
        for s in range(NSEQ):
            for c in range(2):
                sch.dma("sp", uT_d[s, c, :, 0:15], zero_bf[:, 0:15], reads=[zero_b], sembuf=zero_b)
                sch.dma("sp", uT_d[s, c, :, 15 + S:SP], zero_bf[:, 0:15], reads=[zero_b], sembuf=zero_b)

        class LNCtx:
            def __init__(self, es, N, pfx, nslot=2, inplace=False, xh_eng="act"):
                self.N = N
                self.ns = nslot
                self.k = 0
                self.xh_eng = xh_eng
                nch = max(1, N // 512)
                self.nch = nch
                self.z, self.zb = tile(es, pfx + "z", [128, nslot, N], F32, nslot)
                self.st = es.enter_context(nc.sbuf_tensor(uniq(pfx + "st"), [128, nslot, nch, 6], F32))
                self.stb = [[sch.buf(f"{pfx}st{i}{j}") for j in range(nch)] for i in range(nslot)]
                self.sm, self.smb = tile(es, pfx + "sm", [128, nslot, 8], F32, nslot)
                self.smb2 = [sch.buf(pfx + "sm2") for _ in range(nslot)]
                self.smb3 = [sch.buf(pfx + "sm3") for _ in range(nslot)]
                self.smb4 = [sch.buf(pfx + "sm4") for _ in range(nslot)]
                if inplace:
                    self.xh, self.xhb = self.z, self.zb
                else:
                    self.xh, self.xhb = tile(es, pfx + "xh", [128, nslot, N], F32, nslot)

            def slot(self):
                i = self.k % self.ns
                self.k += 1
                return i

            def stages(self, i, g_ap, b_ap, gb_bufs):
                N = self.N
                z = self.z[:, i, :]
                zb = self.zb[i]
                w = min(N, 512)
                st, sm = self.st, self.sm
                xh = self.xh[:, i, :]
                xhb = self.xhb[i]

                def s1():
                    for j in range(self.nch):
                        sch.op("dve", lambda: V.bn_stats(st[:, i, j, :], z[:, j * w:(j + 1) * w]),
                               reads=[zb], writes=[self.stb[i][j]])
                    sch.op("dve", lambda: V.bn_aggr(sm[:, i, 0:2], st[:, i, :, :]),
                           reads=self.stb[i], writes=[self.smb[i]])

                def s2():
                    sch.op("act", lambda: A.activation(out=sm[:, i, 2:3], in_=sm[:, i, 1:2], func=AF.Ln,
                                                       bias=epsc[:], scale=1.0),
                           reads=[self.smb[i], epsc_b], writes=[self.smb2[i]])
                    sch.op("act", lambda: A.activation(out=sm[:, i, 3:4], in_=sm[:, i, 2:3], func=AF.Exp,
                                                       scale=-0.5),
                           reads=[self.smb2[i]], writes=[self.smb3[i]])

                def s3():
                    sch.op("dve", lambda: V.scalar_tensor_tensor(out=sm[:, i, 4:5], in0=sm[:, i, 0:1], scalar=-1.0,
                                                                 in1=sm[:, i, 3:4], op0=ALU.mult, op1=ALU.mult),
                           reads=[self.smb[i], self.smb3[i]], writes=[self.smb4[i]])
                    if self.xh_eng == "act":
                        sch.op("act", lambda: A.activation(out=xh, in_=z, func=AF.Identity, bias=sm[:, i, 4:5],
                                                           scale=sm[:, i, 3:4]),
                               reads=[zb, self.smb3[i], self.smb4[i]], writes=[xhb])
                    else:
                        sch.op("dve", lambda: V.tensor_scalar(out=xh, in0=z, scalar1=sm[:, i, 3:4],
                                                              scalar2=sm[:, i, 4:5], op0=ALU.mult, op1=ALU.add),
                               reads=[zb, self.smb3[i], self.smb4[i]], writes=[xhb])

                def s4():
                    sch.op("dve", lambda: V.tensor_mul(xh, xh, g_ap), reads=[xhb] + gb_bufs, writes=[xhb])
                    sch.op("pool", lambda: G.tensor_add(xh, xh, b_ap), reads=[xhb] + gb_bufs, writes=[xhb])

                return [s1, s2, s3, s4], xh, xhb

            def run(self, i, g_ap, b_ap, gb_bufs):
                sts, xh, xhb = self.stages(i, g_ap, b_ap, gb_bufs)
                for f in sts:
                    f()
                return xh, xhb

        class XTail:
            def __init__(self, es, pfx, bank, nslot=2, cast_eng="act"):
                self.bank = bank
                self.cast_eng = cast_eng
                self.ns = nslot
                self.xbf, self.xbfb = tile(es, pfx + "xbf", [128, nslot, D], BF16, nslot)
                self.xts, self.xtsb = tile(es, pfx + "xts", [128, nslot, 8, 128], BF16, nslot)
                self.k = 0

            def part1(self, xh, xhb, res_ap):
                i = self.k % self.ns
                self.k += 1
                sch.dma("sp", res_ap, xh, reads=[xhb], sembuf=xhb)
                xbf = self.xbf[:, i, :]
                if self.cast_eng == "act":
                    sch.op("act", lambda: A.copy(out=xbf, in_=xh), reads=[xhb], writes=[self.xbfb[i]])
                else:
                    sch.op("pool", lambda: G.tensor_copy(xbf, xh), reads=[xhb], writes=[self.xbfb[i]])
                return i

            def part2(self, i, T_ap, tok0):
                xbf = self.xbf[:, i, :]
                pst = PS[self.bank][:].bitcast(BF16)
                for c in range(8):
                    sch.op("pe", lambda: PE.transpose(pst[:, c * 128:(c + 1) * 128],
                                                      xbf[:, c * 128:(c + 1) * 128], ident[:]),
                           reads=[self.xbfb[i], ident_b], writes=[PB[self.bank]], signal=(c == 7))
                xts = self.xts[:, i, :, :]
                sch.op("dve", lambda: V.tensor_copy(xts, pst[:, 0:1024].rearrange("p (c t) -> p c t", c=8)),
                       reads=[PB[self.bank]], writes=[self.xtsb[i]])
                sch.dma("sp", cpt(T_ap)[:, :, tok0:tok0 + 128], xts, reads=[self.xtsb[i]], sembuf=self.xtsb[i])

            def run(self, xh, xhb, res_ap, T_ap, tok0):
                i = self.part1(xh, xhb, res_ap)
                self.part2(i, T_ap, tok0)

        with ExitStack() as es:
            embt, embt_b = tile(es, "embt", [128, 2, D], F32)
            sch.dma("sp", embt[:], emb_d, writes=[embt_b], sembuf=embt_b)
            ln = LNCtx(es, D, "l0", nslot=8, inplace=True)
            xt = XTail(es, "l0", 7, nslot=4)
            tiles = [(s, t) for s in range(NSEQ) for t in range(NT)]
            groups = [tiles[i:i + 4] for i in range(0, len(tiles), 4)]

            def ld0(g):
                for j, (s, t) in enumerate(groups[g]):
                    i = (g % 2) * 4 + j
                    sch.dma("sp", ln.z[:, i, :], x_d[s, t * 128:(t + 1) * 128, :], writes=[ln.zb[i]],
                            sembuf=ln.zb[i])

            ld0(0)
            for g, grp in enumerate(groups):
                if g + 1 < len(groups):
                    ld0(g + 1)
                chains = []
                for j, (s, t) in enumerate(grp):
                    i = (g % 2) * 4 + j
                    sts, xh, xhb = ln.stages(i, embt[:, 0, :], embt[:, 1, :], [embt_b])
                    hold = {}

                    def p1(xh=xh, xhb=xhb, s=s, t=t, hold=hold):
                        hold["i"] = xt.part1(xh, xhb, xres_d[s, t * 128:(t + 1) * 128, :])

                    def p2(s=s, t=t, hold=hold):
                        xt.part2(hold["i"], xT_d[s], t * 128)

                    chains.append(sts + [p1, p2])
                for k in range(len(chains[0])):
                    for ch in chains:
                        ch[k]()
            sch.barrier()
            if stop == 0:
                raise _Stop()

        for l in range(L):
            lam_init = 0.8 - 0.6 * math.exp(-0.3 * l)
            last = (l == L - 1)
            if True:
                with ExitStack() as les:
                    bincol, bincol_b = tile(les, "bincol", [128, 18], F32)
                    lamv, lamv_b = tile(les, "lamv", [128, 4, 64], F32)
                    lams, lams_b = tile(les, "lams", [128, 2, 64], F32)
                    lamc, lamc_b = tile(les, "lamc", [128, 8], F32)
                    lamc2_b = sch.buf("lamc2")
                    lamc3_b = sch.buf("lamc3")
                    dng, dng_b = tile(les, "dng", [128, 128], F32)
                    sch.dma("sp", bincol[:], bincol_d[:, l, :], writes=[bincol_b], sembuf=bincol_b)
                    sch.dma("sp", lamv[:], lamv_d[:, l, :, :], writes=[lamv_b], sembuf=lamv_b)
                    sch.dma("sp", dng[:], dng_d[:, l, :], writes=[dng_b], sembuf=dng_b)
                    sch.op("dve", lambda: V.tensor_mul(lams[:], lamv[:, 0:2, :], lamv[:, 2:4, :]),
                           reads=[lamv_b], writes=[lams_b])
                    sch.op("dve", lambda: V.reduce_sum(out=lamc[:, 0:2], in_=lams[:], axis=AX.X),
                           reads=[lams_b], writes=[lamc_b])
                    sch.op("act", lambda: A.activation(out=lamc[:, 2:4], in_=lamc[:, 0:2], func=AF.Exp),
                           reads=[lamc_b], writes=[lamc2_b])
                    sch.op("dve", lambda: V.tensor_sub(lamc[:, 4:5], lamc[:, 2:3], lamc[:, 3:4]),
                           reads=[lamc2_b], writes=[lamc3_b])
                    sch.op("dve", lambda: V.tensor_scalar(out=lamc[:, 5:6], in0=lamc[:, 4:5], scalar1=-1.0,
                                                          scalar2=-lam_init, op0=ALU.mult, op1=ALU.add),
                           reads=[lamc3_b], writes=[lamc3_b])
                    neglam = lamc[:, 5:6]
                    sch.op("dve", lambda: V.tensor_scalar_mul(dng[:], dng[:], 1.0 - lam_init), reads=[dng_b],
                           writes=[dng_b])

                    with ExitStack() as es:
                        win, win_b = tile(es, "win", [128, 8, IN_W], BF16)
                        bvt, bvt_b = tile(es, "bvt", [128, 512], F32)
                        sch.dma("sp", bvt[:], bv_d[:, l, :], writes=[bvt_b], sembuf=bvt_b)
                        win_q = [sch.buf(f"winq{q}") for q in range(3)]
                        for q3 in range(3):
                            for c in range(8):
                                sch.dma("pool", win[:, c, q3 * 768:(q3 + 1) * 768],
                                        win_d[l, c * 128:(c + 1) * 128, q3 * 768:(q3 + 1) * 768],
                                        writes=[win_q[q3]], sembuf=win_q[q3])
                        xb, xb_b = tile(es, "xb", [128, 2, 8, 512], BF16, 2)
                        sig, sig_b = tile(es, "sig", [128, 2, 512], F32, 2)
                        ust, ust_b = tile(es, "ust", [128, 2, 2, 512], BF16, 2)
                        qst, qst_b = tile(es, "qst", [128, 2, 4, 512], BF16, 2)
                        kst, kst_b = tile(es, "kst", [128, 2, 4, 512], BF16, 2)
                        qmst, qmst_b = tile(es, "qmst", [128, 2, 2, 512], BF16, 2)
                        vst, vst_b = tile(es, "vst", [128, 2, 4, 512], BF16, 2)
                        cnt = {"ps": 0, "sig": 0}

                        ablocks = [(s_, tb_) for s_ in range(NSEQ) for tb_ in range(NB)]

                        def ldA(bi):
                            s_, tb_ = ablocks[bi]
                            sch.dma("sp", xb[:, bi % 2, :, :], cpt(xT_d[s_])[:, :, tb_ * 512:(tb_ + 1) * 512],
                                    writes=[xb_b[bi % 2]], sembuf=xb_b[bi % 2])

                        ldA(0)
                        for bi, (s, tb) in enumerate(ablocks):
                            if bi + 1 < len(ablocks):
                                ldA(bi + 1)
                            sl = bi % 2
                            t0 = tb * 512

                            def fm_chunk(cc):
                                bk = cnt["ps"] % 4
                                cnt["ps"] += 1
                                for dc in range(8):
                                    sch.op("pe", lambda: PE.matmul(
                                        PS[bk][:], win[:, dc, cc * 128:(cc + 1) * 128], xb[:, sl, dc, :],
                                        start=(dc == 0), stop=(dc == 7)),
                                        reads=[win_q[cc // 6], xb_b[sl]], writes=[PB[bk]], signal=(dc == 7))
                                return bk

                            for c in range(2):
                                bk = fm_chunk(2 + c)
                                si = cnt["sig"] % 2
                                cnt["sig"] += 1
                                sch.op("act", lambda: A.activation(out=sig[:, si, :], in_=PS[bk][:], func=AF.Sigmoid,
                                                                   bias=bincol[:, 2 + c:3 + c], scale=1.0),
                                       reads=[PB[bk], bincol_b], writes=[sig_b[si]])
                                bk2 = fm_chunk(c)
                                sch.op("dve", lambda: V.scalar_tensor_tensor(
                                    out=ust[:, sl, c, :], in0=PS[bk2][:], scalar=bincol[:, c:c + 1], in1=sig[:, si, :],
                                    op0=ALU.add, op1=ALU.mult),
                                    reads=[PB[bk2], bincol_b, sig_b[si]], writes=[ust_b[sl]])
                            sch.dma("sp", cpt(uT_d[s])[:, :, 15 + t0:15 + t0 + 512], ust[:, sl, :, :],
                                    reads=[ust_b[sl]], sembuf=ust_b[sl])
                            for (dst, dstb, cc0, n, dram) in ((qst, qst_b, 4, 4, QT_d), (kst, kst_b, 8, 4, KT_d),
                                                               (qmst, qmst_b, 16, 2, QmT_d)):
                                for h in range(n):
                                    cc = cc0 + h
                                    bk = fm_chunk(cc)
                                    if h % 2 == 0:
                                        sch.op("act", lambda: A.activation(
                                            out=dst[:, sl, h, :], in_=PS[bk][:], func=AF.Identity,
                                            bias=bincol[:, cc:cc + 1], scale=1.0),
                                            reads=[PB[bk], bincol_b], writes=[dstb[sl]])
                                    else:
                                        sch.op("dve", lambda: V.tensor_scalar_add(
                                            dst[:, sl, h, :], PS[bk][:], bincol[:, cc:cc + 1]),
                                            reads=[PB[bk], bincol_b], writes=[dstb[sl]])
                                sch.dma("sp", cpt(dram[s])[:, :, t0:t0 + 512], dst[:, sl, :, :],
                                        reads=[dstb[sl]], sembuf=dstb[sl])
                            for tt in range(4):
                                bk = 4 + tt
                                for dc in range(8):
                                    sch.op("pe", lambda: PE.matmul(
                                        PS[bk][:], xb[:, sl, dc, tt * 128:(tt + 1) * 128], win[:, dc, 1536:2048],
                                        start=(dc == 0), stop=(dc == 7)),
                                        reads=[win_q[2], xb_b[sl]], writes=[PB[bk]], signal=(dc == 7))
                                sch.op("dve", lambda: V.tensor_add(vst[:, sl, tt, :], PS[bk][:], bvt[:]),
                                       reads=[PB[bk], bvt_b], writes=[vst_b[sl]])
                            sch.dma("sp", V_d[s, t0:t0 + 512, :].rearrange("(tt p) n -> p tt n", p=128),
                                    vst[:, sl, :, :], reads=[vst_b[sl]], sembuf=vst_b[sl])
                        sch.barrier()
                        if stop == 1:
                            raise _Stop()

                    for s in range(NSEQ):
                        with ExitStack() as bes:
                            kmT, kmT_b = tile(bes, "kmT", [128, 2, 256], BF16)
                            vma, vma_b = tile(bes, "vma", [128, 2, 4, 65], BF16)
                            btab, btab_b = tile(bes, "btab", [128, 4, 1152], F32)
                            cfar, cfar_b = tile(bes, "cfar", [128, 8], F32)
                            sch.dma("sp", btab[:], btab_d, writes=[btab_b], sembuf=btab_b)
                            sch.op("act", lambda: A.activation(out=btab[:], in_=btab[:], func=AF.Exp),
                                   reads=[btab_b], writes=[btab_b])
                            sch.dma("sp", cfar[:], cfar_d, writes=[cfar_b], sembuf=cfar_b)
                            KT, KT_b = tile(bes, "KT", [128, 4, S], BF16)
                            VA, VA_b = tile(bes, "VA", [128, NT, 4, 129], BF16)
                            wo, wo_b = tile(bes, "wo", [128, 8, D], BF16)
                            vec, vec_b = tile(bes, "vecB", [128, 3, D], F32)
                            sch.op("pool", lambda: G.memset(VA[:, :, :, 128:129], 1.0), writes=[VA_b])
                            for h in range(4):
                                sch.dma("sp", KT[:, h, :], KT_d[s, h], writes=[KT_b], sembuf=KT_b)
                            for t8 in range(0, NT, 8):
                                n8 = min(8, NT - t8)
                                for h in range(4):
                                    sch.dma("sp", VA[:, t8:t8 + n8, h, 0:128],
                                            V_d[s, t8 * 128:(t8 + n8) * 128, h * 128:(h + 1) * 128].rearrange(
                                                "(t p) d -> p t d", p=128),
                                            writes=[VA_b], sembuf=VA_b)
                            for c in range(8):
                                sch.dma("pool", wo[:, c, :], wout_d[l, c * 128:(c + 1) * 128, :], writes=[wo_b],
                                        sembuf=wo_b)
                            sch.dma("sp", vec[:], vec_d[:, l, 0:3, :], writes=[vec_b], sembuf=vec_b)
                            with ExitStack() as es:
                                wkv, wkv_b = tile(es, "wkv", [128, 8, 512], BF16)
                                cwcol, cwcol_b = tile(es, "cwcol", [128, 2, CONV_W], F32)
                                cvec, cvec_b = tile(es, "cvec", [128, 3, 256], F32)
                                sch.dma("sp", cwcol[:], cwcol_d[:, l, :, :], writes=[cwcol_b], sembuf=cwcol_b)
                                sch.dma("sp", cvec[:], cvec_d[:, l, :, :], writes=[cvec_b], sembuf=cvec_b)
                                for c2 in range(4):
                                    sch.dma("pool", wkv[:, 2 * c2:2 * c2 + 2, :],
                                            wkv_d[l, c2 * 256:(c2 + 1) * 256, :].rearrange("(c p) n -> p c n", p=128),
                                            writes=[wkv_b], sembuf=wkv_b)
                                memb, memb_b = tile(es, "memb", [128, 2, D], BF16)
                                memT, memT_b = tile(es, "memT", [128, 8, 256], BF16)
                                sch.dma("pool", memb[:], mem_d[s].rearrange("(a p) n -> p a n", p=128), writes=[memb_b],
                                        sembuf=memb_b)
                                for a in range(2):
                                    pst = PS[a][:].bitcast(BF16)
                                    for c in range(8):
                                        sch.op("pe", lambda: PE.transpose(
                                            pst[:, c * 128:(c + 1) * 128], memb[:, a, c * 128:(c + 1) * 128], ident[:]),
                                            reads=[memb_b, ident_b], writes=[PB[a]], signal=(c == 7))
                                    sch.op("dve", lambda: V.tensor_copy(
                                        memT[:, :, a * 128:(a + 1) * 128],
                                        pst[:, 0:1024].rearrange("p (c t) -> p c t", c=8)),
                                        reads=[PB[a]], writes=[memT_b])
                                for c in range(2):
                                    bk = 2 + c
                                    for dc in range(8):
                                        sch.op("pe", lambda: PE.matmul(
                                            PS[bk][:, 0:256], wkv[:, dc, c * 128:(c + 1) * 128], memT[:, dc, :],
                                            start=(dc == 0), stop=(dc == 7)),
                                            reads=[wkv_b, memT_b], writes=[PB[bk]], signal=(dc == 7))
                                    sch.op("act", lambda: A.copy(out=kmT[:, c, :], in_=PS[bk][:, 0:256]),
                                           reads=[PB[bk]], writes=[kmT_b])
                                sch.op("pool", lambda: G.memset(vma[:], 1.0), writes=[vma_b])
                                for a in range(2):
                                    bk = 4 + a
                                    for dc in range(8):
                                        sch.op("pe", lambda: PE.matmul(
                                            PS[bk][:, 0:256], memT[:, dc, a * 128:(a + 1) * 128], wkv[:, dc, 256:512],
                                            start=(dc == 0), stop=(dc == 7)),
                                            reads=[wkv_b, memT_b], writes=[PB[bk]], signal=(dc == 7))
                                    sch.op("dve", lambda: V.tensor_copy(
                                        vma[:, a, :, 0:64], PS[bk][:, 0:256].rearrange("p (h d) -> p h d", h=4)),
                                        reads=[PB[bk]], writes=[vma_b])

                                diag, diag_b = tile(es, "diag", [128, 2, CONV_W, 128], BF16)
                                dgb = sch.buf("dg")
                                for c in range(2):
                                    for j in range(CONV_W):
                                        if j % 2:
                                            sch.op("pool", lambda: G.tensor_scalar_mul(
                                                diag[:, c, j, :], ident_f[:], cwcol[:, c, j:j + 1]),
                                                reads=[identf_b, cwcol_b], writes=[sch.buf("dgx")])
                                        else:
                                            sch.op("dve", lambda: V.tensor_scalar_mul(
                                                diag[:, c, j, :], ident_f[:], cwcol[:, c, j:j + 1]),
                                                reads=[identf_b, cwcol_b], writes=[sch.buf("dgx")])
                                sch.op("dve", lambda: V.memset(scr[:, 0:1], 0.0), writes=[diag_b])
                                sch.op("pool", lambda: G.memset(scr[:, 1:2], 0.0), writes=[diag_b])
                                ub, ub_b = tile(es, "ub", [128, 2, 2, 542], BF16, 2)
                                lnc = LNCtx(es, 256, "lc", nslot=8, inplace=True)
                                ex, ex_b = tile(es, "cex", [128, 4, 256], F32, 4)
                                cst, cst_b = tile(es, "cst", [128, 2, 4, 256], BF16, 2)

                                def ldU(tb):
                                    sch.dma("sp", ub[:, tb % 2, :, :], cpt(uT_d[s])[:, :, tb * 512:tb * 512 + 542],
                                            writes=[ub_b[tb % 2]], sembuf=ub_b[tb % 2])

                                def conv_mm(tb):
                                    sl = tb % 2
                                    for tt in range(4):
                                        bk = 4 + tt
                                        for c in range(2):
                                            for j in range(CONV_W):
                                                sch.op("pe", lambda: PE.matmul(
                                                    PS[bk][:, c * 128:(c + 1) * 128],
                                                    ub[:, sl, c, tt * 128 + j:tt * 128 + j + 128], diag[:, c, j, :],
                                                    start=(c == 0 and j == 0), stop=(j == CONV_W - 1),
                                                    skip_group_check=True),
                                                    reads=[ub_b[sl], diag_b], writes=[PB[bk]],
                                                    signal=(c == 1 and j == CONV_W - 1))
                                        i = sl * 4 + tt
                                        sch.op("dve", lambda: V.tensor_add(lnc.z[:, i, :], PS[bk][:, 0:256],
                                                                           cvec[:, 0, :]),
                                               reads=[PB[bk], cvec_b], writes=[lnc.zb[i]])

                                def conv_chains(tb):
                                    sl = tb % 2
                                    chains = []
                                    for tt in range(4):
                                        i = sl * 4 + tt
                                        sts, xh, xhb = lnc.stages(i, cvec[:, 1, :], cvec[:, 2, :], [cvec_b])

                                        def s5(xh=xh, xhb=xhb, tt=tt):
                                            sch.op("act", lambda: A.activation(out=ex[:, tt, :], in_=xh, func=AF.Exp,
                                                                               scale=-1.0),
                                                   reads=[xhb], writes=[ex_b[tt]])

                                        def s6(tt=tt):
                                            sch.op("pool", lambda: G.tensor_scalar_add(ex[:, tt, :], ex[:, tt, :], 1.0),
                                                   reads=[ex_b[tt]], writes=[ex_b[tt]])

                                        def s7(xh=xh, xhb=xhb, tt=tt):
                                            sch.op("dve", lambda: V.reciprocal(ex[:, tt, :], ex[:, tt, :]),
                                                   reads=[ex_b[tt]], writes=[ex_b[tt]])
                                            sch.op("dve", lambda: V.tensor_mul(cst[:, sl, tt, :], xh, ex[:, tt, :]),
                                                   reads=[ex_b[tt], xhb], writes=[cst_b[sl]])

                                        chains.append(sts + [s5, s6, s7])
                                    for k in range(len(chains[0])):
                                        for ch in chains:
                                            ch[k]()
                                    sch.dma("sp", cv_d[s, tb * 512:(tb + 1) * 512, :].rearrange("(tt p) n -> p tt n",
                                                                                                p=128),
                                            cst[:, sl, :, :], reads=[cst_b[sl]], sembuf=cst_b[sl])

                                ldU(0)
                                if NB > 1:
                                    ldU(1)
                                conv_mm(0)
                                for tb in range(NB):
                                    if tb + 1 < NB:
                                        conv_mm(tb + 1)
                                    if tb + 2 < NB:
                                        ldU(tb + 2)
                                    conv_chains(tb)
                                sch.barrier()
                                if stop == 2:
                                    raise _Stop()

                            with ExitStack() as es:
                                qtb, qtb_b = tile(es, "qtb", [128, 2, 4, 512], BF16, 2)
                                qmb, qmb_b = tile(es, "qmb", [128, 2, 2, 512], BF16, 2)
                                mixed = es.enter_context(nc.sbuf_tensor(uniq("mixed"), [128, 4, D], BF16))
                                mx_b = [[sch.buf(f"mx{tt}{c}") for c in range(8)] for tt in range(4)]
                                PT, PT_b = tile(es, "PT", [128, 6, 512], BF16, 6)
                                tmp, tmp_b = tile(es, "tmpb", [128, 4, 512], F32, 4)
                                ep, ep_b = tile(es, "ep", [128, 4, 8], F32, 4)
                                oacc = es.enter_context(nc.sbuf_tensor(uniq("oacc"), [128, 2, 8, 129], F32))
                                oacc_b = [[sch.buf(f"oacc{i}{b}") for b in range(3)] for i in range(2)]
                                ew, ew_b = tile(es, "ew", [128, 2, 4, 128], F32, 2)
                                ew2, ew2_b = tile(es, "ew2", [128, 2, 4, 128], F32, 2)
                                esm, esm_b = tile(es, "esm", [128, 2, 16], F32, 2)
                                esm2_b = [sch.buf("esm2") for _ in range(2)]
                                esm3_b = [sch.buf("esm3") for _ in range(2)]
                                esn, esn_b = tile(es, "esn", [128, 2, 8], F32, 2)
                                pend = []
                                mixT, mixT_b = tile(es, "mixT", [128, 2, 8, 128], BF16, 2)
                                xin, xin_b = tile(es, "xin", [128, 2, D], F32, 2)
                                ln1 = LNCtx(es, D, "l1", nslot=4, inplace=True, xh_eng="dve")
                                xt1 = XTail(es, "l1", 7, cast_eng="pool")
                                bg = []
                                cnt = {"pt": 0, "mpt": 0, "tmp": 0, "tile": 0, "oa": 0}

                                def ldQ(qb):
                                    sl = qb % 2
                                    sch.dma("sp", qtb[:, sl, :, :], cpt(QT_d[s])[:, :, qb * 512:(qb + 1) * 512],
                                            writes=[qtb_b[sl]], sembuf=qtb_b[sl])
                                    sch.dma("sp", qmb[:, sl, :, :], cpt(QmT_d[s])[:, :, qb * 512:(qb + 1) * 512],
                                            writes=[qmb_b[sl]], sembuf=qmb_b[sl])

                                def ldX(k):
                                    xi = k % 2
                                    sch.dma("sp", xin[:, xi, :], xres_d[s, k * 128:(k + 1) * 128, :], writes=[xin_b[xi]],
                                            sembuf=xin_b[xi])

                                ones_bf, ones_b = tile(es, "ones_bf", [1, 128], BF16)
                                bout_bf, bout_b = tile(es, "bout_bf", [1, D], BF16)
                                sch.op("pool", lambda: G.memset(ones_bf[:], 1.0), writes=[ones_b])
                                sch.op("pool", lambda: G.tensor_copy(bout_bf[:], vec[0:1, 0, :]), reads=[vec_b],
                                       writes=[bout_b])
                                ldQ(0)
                                ldX(0)
                                for qb in range(NB):
                                    sl = qb % 2
                                    q0 = qb * 512
                                    if qb + 1 < NB:
                                        ldQ(qb + 1)
                                    for tt in range(4):
                                        sch.dma("sp", mixed[:, tt, 0:256], cv_d[s, q0 + tt * 128:q0 + (tt + 1) * 128, :],
                                                writes=[mx_b[tt][0], mx_b[tt][1]], sembuf=mx_b[tt][0])
                                    if 'm' in os.environ.get('BPARTS', 'mdo'):
                                        def m_qk(h):
                                            pr = slice((h % 2) * 64, (h % 2) * 64 + 64)
                                            hc = h // 2
                                            b0 = (h % 2) * 2
                                            for mt in range(2):
                                                sch.op("pe", lambda: PE.matmul(
                                                    PS[b0 + mt][:], kmT[pr, hc, mt * 128:(mt + 1) * 128],
                                                    qmb[pr, sl, hc, :], start=True, stop=True),
                                                    reads=[kmT_b, qmb_b[sl]], writes=[PB[b0 + mt]])

                                        def m_ex(h):
                                            b0 = (h % 2) * 2
                                            pi = (cnt["mpt"] % 3) * 2
                                            cnt["mpt"] += 1
                                            sch.op("act", lambda: A.activation(
                                                out=PT[:, pi:pi + 2, :].rearrange("p a b -> p (a b)"),
                                                in_=psall[:, b0 * 512:(b0 + 2) * 512], func=AF.Exp, scale=0.125),
                                                reads=[PB[b0], PB[b0 + 1]], writes=[PT_b[pi], PT_b[pi + 1]])
                                            return [pi, pi + 1]

                                        def m_pv(h, pts):
                                            ab = 4 + (h % 2)
                                            for qt in range(4):
                                                for mt in range(2):
                                                    sch.op("pe", lambda: PE.matmul(
                                                        PS[ab][:, qt * 65:qt * 65 + 65],
                                                        PT[:, pts[mt], qt * 128:(qt + 1) * 128], vma[:, mt, h, :],
                                                        start=(qt == 0 and mt == 0), stop=(mt == 1),
                                                        skip_group_check=True),
                                                        reads=[PT_b[pts[mt]], vma_b], writes=[PB[ab]],
                                                        signal=(qt == 3 and mt == 1))
                                            ei = h
                                            sch.op("dve", lambda: V.reciprocal(
                                                ep[:, ei, 0:4],
                                                PS[ab][:, 0:260].rearrange("p (q e) -> p q e", e=65)[:, :, 64]),
                                                reads=[PB[ab]], writes=[ep_b[ei]])
                                            sch.op("dve", lambda: V.tensor_mul(
                                                mixed[:, :, 768 + h * 64:768 + (h + 1) * 64],
                                                PS[ab][:, 0:260].rearrange("p (q e) -> p q e", e=65)[:, :, 0:64],
                                                ep[:, ei, 0:4].unsqueeze(2).to_broadcast([128, 4, 64])),
                                                reads=[PB[ab], ep_b[ei]], writes=[mx_b[qt][6 + h // 2] for qt in range(4)])

                                        m_qk(0)
                                        for h in range(4):
                                            if h + 1 < 4:
                                                m_qk(h + 1)
                                            pts = m_ex(h)
                                            m_pv(h, pts)
                                    steps = [(h, kt) for h in range(4) for kt in range(NT)]
                                    sched_epi = {}

                                    def qk(i):
                                        h, kt = steps[i]
                                        for m in range(2):
                                            bk = (kt % 2) * 2 + m
                                            pr = slice(m * 64, m * 64 + 64)
                                            sch.op("pe", lambda: PE.matmul(
                                                PS[bk][:], KT[pr, h, kt * 128:(kt + 1) * 128], qtb[pr, sl, h, :],
                                                start=True, stop=True),
                                                reads=[KT_b, qtb_b[sl]], writes=[PB[bk]])

                                    def ex(i):
                                        h, kt = steps[i]
                                        off = kt * 128 - q0
                                        near = (-256 < off < 640)
                                        b0 = (kt % 2) * 2
                                        pi = (cnt["pt"] % 3) * 2
                                        cnt["pt"] += 1
                                        pis = [pi, pi + 1]
                                        pt2 = PT[:, pi:pi + 2, :].rearrange("p a b -> p (a b)")
                                        if near:
                                            ts_ = (cnt["tmp"] % 2) * 2
                                            cnt["tmp"] += 1
                                            sch.op("act", lambda: A.activation(
                                                out=tmp[:, ts_:ts_ + 2, :].rearrange("p a b -> p (a b)"),
                                                in_=psall[:, b0 * 512:(b0 + 2) * 512], func=AF.Exp, scale=0.125),
                                                reads=[PB[b0], PB[b0 + 1]], writes=[tmp_b[ts_], tmp_b[ts_ + 1]])
                                            sch.op("dve", lambda: V.tensor_mul(
                                                PT[:, pi:pi + 2, :], tmp[:, ts_:ts_ + 2, :],
                                                btab[:, h, 512 - off:1024 - off].unsqueeze(1).to_broadcast(
                                                    [128, 2, 512])),
                                                reads=[tmp_b[ts_], tmp_b[ts_ + 1], btab_b],
                                                writes=[PT_b[pi], PT_b[pi + 1]])
                                        else:
                                            ci = h * 2 + (1 if off > 0 else 0)
                                            sch.op("act", lambda: A.activation(
                                                out=pt2, in_=psall[:, b0 * 512:(b0 + 2) * 512], func=AF.Exp,
                                                bias=cfar[:, ci:ci + 1], scale=0.125),
                                                reads=[PB[b0], PB[b0 + 1], cfar_b], writes=[PT_b[pi], PT_b[pi + 1]])
                                        return pis

                                    def pv(i, pis):
                                        h, kt = steps[i]
                                        for m in range(2):
                                            for qt in range(4):
                                                idx = m * 4 + qt
                                                bk = 4 + idx // 3
                                                col = (idx % 3) * 129
                                                sch.op("pe", lambda: PE.matmul(
                                                    PS[bk][:, col:col + 129],
                                                    PT[:, pis[m], qt * 128:(qt + 1) * 128], VA[:, kt, h, :],
                                                    start=(kt == 0 and (idx % 3) == 0), stop=(kt == NT - 1),
                                                    skip_group_check=True),
                                                    reads=[PT_b[pis[m]], VA_b], writes=[PB[bk]],
                                                    signal=(idx % 3 == 2 or idx == 7))

                                    def evac(h):
                                        oi = cnt["oa"] % 2
                                        cnt["oa"] += 1
                                        for b in range(3):
                                            n = 3 if b < 2 else 2
                                            sch.op("dve", lambda: V.tensor_copy(
                                                oacc[:, oi, b * 3:b * 3 + n, :],
                                                PS[4 + b][:, 0:n * 129].rearrange("p (a c) -> p a c", c=129)),
                                                reads=[PB[4 + b]], writes=[oacc_b[oi][b]])
                                        return oi

                                    def epi1(h, oi):
                                        en = "dve" if h == 3 else "pool"
                                        E_ = V if h == 3 else G
                                        sch.op("dve", lambda: V.reciprocal(esm[:, oi, 0:8], oacc[:, oi, :, 128]),
                                               reads=oacc_b[oi], writes=[esm_b[oi]])
                                        sch.op(en, lambda: E_.tensor_scalar_mul(esm[:, oi, 8:12], esm[:, oi, 4:8],
                                                                                   neglam),
                                               reads=[esm_b[oi], lamc3_b], writes=[esm2_b[oi]])
                                        sch.op(en, lambda: E_.tensor_mul(
                                            ew[:, oi, :, :], oacc[:, oi, 0:4, 0:128],
                                            esm[:, oi, 0:4].unsqueeze(2).to_broadcast([128, 4, 128])),
                                            reads=oacc_b[oi] + [esm_b[oi]], writes=[ew_b[oi]])
                                        sch.op(en, lambda: E_.tensor_mul(
                                            ew2[:, oi, :, :], oacc[:, oi, 4:8, 0:128],
                                            esm[:, oi, 8:12].unsqueeze(2).to_broadcast([128, 4, 128])),
                                            reads=oacc_b[oi] + [esm2_b[oi]], writes=[ew2_b[oi]])
                                        sch.op(en, lambda: E_.tensor_add(ew[:, oi, :, :], ew[:, oi, :, :],
                                                                            ew2[:, oi, :, :]),
                                               reads=[ew_b[oi], ew2_b[oi]], writes=[ew_b[oi]])
                                        sch.op(en, lambda: E_.tensor_mul(ew2[:, oi, :, :], ew[:, oi, :, :],
                                                                            ew[:, oi, :, :]),
                                               reads=[ew_b[oi]], writes=[ew2_b[oi]])

                                    def epi1b(h, oi):
                                        sch.op("dve", lambda: V.reduce_sum(out=esm[:, oi, 12:16], in_=ew2[:, oi, :, :],
                                                                           axis=AX.X),
                                               reads=[ew2_b[oi]], writes=[esm3_b[oi]])

                                    def epi2(h, oi):
                                        sch.op("act", lambda: A.activation(
                                            out=esn[:, oi, 0:4], in_=esm[:, oi, 12:16], func=AF.Ln, bias=epsc[:],
                                            scale=1.0 / 128.0),
                                            reads=[esm3_b[oi], epsc_b], writes=[esn_b[oi]])
                                        sch.op("act", lambda: A.activation(
                                            out=esn[:, oi, 4:8], in_=esn[:, oi, 0:4], func=AF.Exp, scale=-0.5),
                                            reads=[esn_b[oi]], writes=[esn_b[oi]])

                                    def epi3(h, oi):
                                        en = "dve" if h == 3 else "pool"
                                        E_ = V if h == 3 else G
                                        sch.op(en, lambda: E_.tensor_mul(
                                            ew[:, oi, :, :], ew[:, oi, :, :],
                                            esn[:, oi, 4:8].unsqueeze(2).to_broadcast([128, 4, 128])),
                                            reads=[ew_b[oi], esn_b[oi]], writes=[ew_b[oi]])
                                        sch.op(en, lambda: E_.tensor_mul(
                                            mixed[:, :, 256 + h * 128:256 + (h + 1) * 128], ew[:, oi, :, :],
                                            dng[:].unsqueeze(1).to_broadcast([128, 4, 128])),
                                            reads=[ew_b[oi], dng_b], writes=[mx_b[qt][2 + h] for qt in range(4)])

                                    def run_sched(i):
                                        for fn in sched_epi.pop(i, []):
                                            fn()

                                    if 'd' in os.environ.get('BPARTS', 'mdo'):
                                        def head_end(i):
                                            h = steps[i][0]
                                            for j in sorted(sched_epi):
                                                for fn in sched_epi[j]:
                                                    fn()
                                            sched_epi.clear()
                                            oi = evac(h)
                                            sched_epi.setdefault(i + 3, []).append(lambda h=h, oi=oi: epi1(h, oi))
                                            sched_epi.setdefault(i + 11, []).append(lambda h=h, oi=oi: epi1b(h, oi))
                                            sched_epi.setdefault(i + 14, []).append(lambda h=h, oi=oi: epi2(h, oi))
                                            sched_epi.setdefault(i + 17, []).append(lambda h=h, oi=oi: epi3(h, oi))

                                        qk(0)
                                        prev = None
                                        for i, (h, kt) in enumerate(steps):
                                            if i + 1 < len(steps):
                                                qk(i + 1)
                                            pis = ex(i)
                                            if prev is not None:
                                                pv(i - 1, prev)
                                                if steps[i - 1][1] == NT - 1:
                                                    head_end(i - 1)
                                            prev = pis
                                            if i >= 2 and (i % 2 == 0) and bg:
                                                bg.pop(0)()
                                            run_sched(i)
                                        pv(len(steps) - 1, prev)
                                        head_end(len(steps) - 1)
                                        for i in sorted(sched_epi):
                                            for fn in sched_epi[i]:
                                                fn()
                                        sched_epi.clear()
                                    for fn in pend:
                                        fn()
                                    pend.clear()
                                    pst = PS[7][:].bitcast(BF16)

                                    def mixT_of(tt, k):
                                        mi = k % 2
                                        for c in range(8):
                                            sch.op("pe", lambda: PE.transpose(
                                                pst[:, c * 128:(c + 1) * 128], mixed[:, tt, c * 128:(c + 1) * 128],
                                                ident[:]),
                                                reads=[mx_b[tt][c], ident_b], writes=[PB[7]], signal=(c == 7))
                                        sch.op("dve", lambda: V.tensor_copy(
                                            out=mixT[:, mi, :, :], in_=pst[:, 0:1024].rearrange("p (c t) -> p c t", c=8)),
                                            reads=[PB[7]], writes=[mixT_b[mi]])

                                    if 'o' in os.environ.get('BPARTS', 'mdo'):
                                        for fn in bg:
                                            fn()
                                        bg.clear()
                                        mixT_of(0, cnt["tile"])
                                        chains = []
                                        for tt in range(4):
                                            k = cnt["tile"]
                                            cnt["tile"] += 1
                                            mi = k % 2
                                            xi = k % 2
                                            tok0 = q0 + tt * 128
                                            if k + 1 < NT:
                                                ldX(k + 1)
                                            if tt + 1 < 4:
                                                mixT_of(tt + 1, k + 1)
                                            zi = ln1.slot()
                                            for hf in range(2):
                                                bk = (tt % 3) * 2 + hf
                                                sch.op("pe", lambda: PE.matmul(
                                                    PS[bk][:], ones_bf[:], bout_bf[:, hf * 512:(hf + 1) * 512],
                                                    start=True, stop=False),
                                                    reads=[ones_b, bout_b], writes=[PB[bk]], signal=False)
                                                for c in range(8):
                                                    sch.op("pe", lambda: PE.matmul(
                                                        PS[bk][:], mixT[:, mi, c, :], wo[:, c, hf * 512:(hf + 1) * 512],
                                                        start=False, stop=(c == 7)),
                                                        reads=[mixT_b[mi], wo_b], writes=[PB[bk]], signal=(c == 7))
                                                sch.op("dve", lambda: V.scalar_tensor_tensor(
                                                    out=ln1.z[:, zi, hf * 512:(hf + 1) * 512],
                                                    in0=xin[:, xi, hf * 512:(hf + 1) * 512], scalar=ALPHA, in1=PS[bk][:],
                                                    op0=ALU.mult, op1=ALU.add),
                                                    reads=[PB[bk], xin_b[xi]], writes=[ln1.zb[zi]])
                                            sts, xh, xhb = ln1.stages(zi, vec[:, 1, :], vec[:, 2, :], [vec_b])
                                            hold = {}

                                            def p1(xh=xh, xhb=xhb, tok0=tok0, hold=hold):
                                                hold["i"] = xt1.part1(xh, xhb, x1res_d[s, tok0:tok0 + 128, :])

                                            def p2(tok0=tok0, hold=hold):
                                                xt1.part2(hold["i"], x1T_d[s], tok0)

                                            chains.append(sts + [p1, p2])
                                        for a, b in ((0, 1), (2, 3)):
                                            for fa, fb in zip(chains[a], chains[b]):
                                                bg.append(fa)
                                                bg.append(fb)
                                for fn in bg:
                                    fn()
                                bg.clear()
                                for fn in pend:
                                    fn()
                                pend.clear()
                                sch.barrier()
                                if stop == 3:
                                    raise _Stop()

            with ExitStack() as es:
                wup, wup_b = tile(es, "wup", [128, 8, D_FF], BF16)
                wdn, wdn_b = tile(es, "wdn", [128, 32, D], BF16)
                vec, vec_b = tile(es, "vecE", [128, 2, D], F32)
                wup_q = [sch.buf(f"wupq{q}") for q in range(4)]
                wdn_q = [sch.buf(f"wdnq{q}") for q in range(4)]
                for q4 in range(4):
                    for c in range(8):
                        sch.dma("pool", wup[:, c, q4 * 1024:(q4 + 1) * 1024],
                                wup_d[l, c * 128:(c + 1) * 128, q4 * 1024:(q4 + 1) * 1024], writes=[wup_q[q4]],
                                sembuf=wup_q[q4])
                for a in range(32):
                    sch.dma("pool", wdn[:, a, :], wdn_d[l, a * 128:(a + 1) * 128, :], writes=[wdn_q[a // 8]],
                            sembuf=wdn_q[a // 8])
                sch.dma("sp", vec[:], vec_d[:, l, 3:5, :], writes=[vec_b], sembuf=vec_b)
                xb, xb_b = tile(es, "xbE", [128, 8, 512], BF16)
                hT = es.enter_context(nc.sbuf_tensor(uniq("hT"), [128, 32, 512], BF16))
                hT_b = [sch.buf(f"hT{f}") for f in range(32)]
                rl, rl_b = tile(es, "rl", [128, 2, 512], F32, 2)
                xin, xin_b = tile(es, "xinE", [128, 1, D], F32, 1)
                ln2 = LNCtx(es, D, "l2", nslot=1)
                xt2 = XTail(es, "l2", 3, nslot=1)
                cnt = {"up": 0}
                pend = []
                blocks = [(s, tb) for s in range(NSEQ) for tb in range(NB)]

                def ldE(bi):
                    s_, tb_ = blocks[bi]
                    sch.dma("sp", xb[:], cpt(x1T_d[s_])[:, :, tb_ * 512:(tb_ + 1) * 512], writes=[xb_b], sembuf=xb_b)

                ldE(0)
                for bi, (s, tb) in enumerate(blocks):
                    t0 = tb * 512
                    for f in range(32):
                        bk = cnt["up"] % 3
                        ri = cnt["up"] % 2
                        cnt["up"] += 1
                        for dc in range(8):
                            sch.op("pe", lambda: PE.matmul(
                                PS[bk][:], wup[:, dc, f * 128:(f + 1) * 128], xb[:, dc, :],
                                start=(dc == 0), stop=(dc == 7)),
                                reads=[wup_q[f // 8], xb_b], writes=[PB[bk]], signal=(dc == 7))
                        sch.op("act", lambda: A.activation(out=rl[:, ri, :], in_=PS[bk][:], func=AF.Relu),
                               reads=[PB[bk]], writes=[rl_b[ri]])
                        if f % 4:
                            sch.op("dve", lambda: V.tensor_mul(hT[:, f, :], rl[:, ri, :], rl[:, ri, :]),
                                   reads=[rl_b[ri]], writes=[hT_b[f]])
                        else:
                            sch.op("pool", lambda: G.tensor_mul(hT[:, f, :], rl[:, ri, :], rl[:, ri, :]),
                                   reads=[rl_b[ri]], writes=[hT_b[f]])
                        if f == 6:
                            for fn in pend:
                                fn()
                            pend.clear()
                    if bi + 1 < len(blocks):
                        ldE(bi + 1)
                    for tt in range(4):
                        tok0 = t0 + tt * 128
                        sch.dma("sp", xin[:, 0, :], x1res_d[s, tok0:tok0 + 128, :], writes=[xin_b[0]],
                                sembuf=xin_b[0])
                        zi = ln2.slot()
                        for hf in range(2):
                            bk = 4 + (tt % 2) * 2 + hf
                            for f in range(32):
                                sch.op("pe", lambda: PE.matmul(
                                    PS[bk][:], hT[:, f, tt * 128:(tt + 1) * 128],
                                    wdn[:, f, hf * 512:(hf + 1) * 512], start=(f == 0), stop=(f == 31)),
                                    reads=[hT_b[f], wdn_q[f // 8]], writes=[PB[bk]], signal=(f == 31))
                            sch.op("dve", lambda: V.scalar_tensor_tensor(
                                out=ln2.z[:, zi, hf * 512:(hf + 1) * 512],
                                in0=xin[:, 0, hf * 512:(hf + 1) * 512], scalar=ALPHA, in1=PS[bk][:],
                                op0=ALU.mult, op1=ALU.add),
                                reads=[PB[bk], xin_b[0]], writes=[ln2.zb[zi]])
                        for fn in pend:
                            fn()
                        pend.clear()
                        xh, xhb = ln2.run(zi, vec[:, 0, :], vec[:, 1, :], [vec_b])
                        dst = out_d if last else xres_d
                        xi2 = xt2.part1(xh, xhb, dst[s, tok0:tok0 + 128, :])
                        if not last:
                            pend.append(lambda xi2=xi2, s=s, tok0=tok0: xt2.part2(xi2, xT_d[s], tok0))
                for fn in pend:
                    fn()
                pend.clear()
                sch.barrier()
                if stop == 4:
                    raise _Stop()
    return nc


def host_inputs(S, L, NSEQ, inp, core):
    f = np.float32
    b0 = core * NSEQ
    rel = (np.arange(128)[:, None] - np.arange(1152)[None, :] + 512)
    bk = t5_bucket_np(rel)
    rb = np.asarray(inp["rel_bias"], f)
    btab = np.ascontiguousarray(rb[bk].transpose(0, 2, 1))
    cfar = np.empty((128, 8), f)
    for h in range(4):
        cfar[:, 2 * h] = rb[15, h]
        cfar[:, 2 * h + 1] = rb[31, h]
    rep = lambda a: np.ascontiguousarray(np.broadcast_to(np.asarray(a, f)[None], (128,) + tuple(np.shape(a))))
    b_in = np.asarray(inp["b_in"], f)[:L]
    m = {
        "x": np.ascontiguousarray(np.asarray(inp["x"], f)[b0:b0 + NSEQ]),
        "mem": np.ascontiguousarray(np.asarray(inp["mem"], f)[b0:b0 + NSEQ]),
        "emb": rep(np.stack([inp["emb_ln_g"], inp["emb_ln_b"]])),
        "btab": btab, "cfar": cfar,
        "w_in": np.ascontiguousarray(np.asarray(inp["w_in"], f)[:L]),
        "w_mem_kv": np.ascontiguousarray(np.asarray(inp["w_mem_kv"], f)[:L]),
        "w_out": np.ascontiguousarray(np.asarray(inp["w_out"], f)[:L]),
        "w_up": np.ascontiguousarray(np.asarray(inp["w_up"], f)[:L]),
        "w_down": np.ascontiguousarray(np.asarray(inp["w_down"], f)[:L]),
        "bincol": np.ascontiguousarray(b_in.reshape(L, 18, 128).transpose(2, 0, 1)),
        "bvt": rep(b_in[:, 1536:2048]),
        "cwcol": np.ascontiguousarray(np.asarray(inp["conv_w"], f)[:L].reshape(L, CONV_W, 2, 128).transpose(3, 0, 2, 1)),
        "cvec": rep(np.stack([np.asarray(inp["conv_b"], f)[:L], np.asarray(inp["conv_ln_g"], f)[:L],
                              np.asarray(inp["conv_ln_b"], f)[:L]], axis=1)),
        "lamv": rep(np.stack([np.asarray(inp["lambda_q1"], f)[:L], np.asarray(inp["lambda_q2"], f)[:L],
                              np.asarray(inp["lambda_k1"], f)[:L], np.asarray(inp["lambda_k2"], f)[:L]], axis=1)),
        "dng": rep(np.asarray(inp["diff_norm_g"], f)[:L]),
        "vec": rep(np.stack([np.asarray(inp["b_out"], f)[:L], np.asarray(inp["ln1_g"], f)[:L],
                             np.asarray(inp["ln1_b"], f)[:L], np.asarray(inp["ln2_g"], f)[:L],
                             np.asarray(inp["ln2_b"], f)[:L]], axis=1)),
    }
    return m


_NC_CACHE = {}


def kernel(**inputs):
    x = np.asarray(inputs["x"])
    B, S, _ = x.shape
    L = DEPTH
    NSEQ = B // NCORES
    key = (S, L, NSEQ)
    if key not in _NC_CACHE:
        _NC_CACHE[key] = build(S, L, NSEQ)
    nc = _NC_CACHE[key]
    in_maps = [host_inputs(S, L, NSEQ, inputs, c) for c in range(NCORES)]
    res = run_bass_kernel_spmd(nc, in_maps, core_ids=list(range(NCORES)))
    out = np.concatenate([np.asarray(r["out"]) for r in res.results], axis=0)
    return out.astype(np.float32)
```

```python
import math
from contextlib import ExitStack

import numpy as np
import concourse.bass as bass
import concourse.mybir as mybir
from concourse.bass_utils import run_bass_kernel_spmd

F32 = mybir.dt.float32
BF16 = mybir.dt.bfloat16
AF = mybir.ActivationFunctionType
ALU = mybir.AluOpType
AX = mybir.AxisListType

D = 1024
DEPTH = 4
MEM_LEN = 256
CONV_W = 31
IN_W = 2304
D_FF = 4096
ALPHA = (2.0 * DEPTH) ** 0.25
LN_EPS = 1e-5
NCORES = 8


class Buf:
    __slots__ = ("name", "w", "r", "excl", "sem")

    def __init__(self, name, excl=False):
        self.name = name
        self.w = None
        self.r = {}
        self.excl = excl
        self.sem = None


class Sched:
    def __init__(self, nc, es):
        self.nc = nc
        self.h = {"pe": nc.tensor, "act": nc.scalar, "dve": nc.vector, "pool": nc.gpsimd, "sp": nc.sync}
        self.sem = {e: es.enter_context(nc.semaphore("s_" + e)) for e in ("pe", "act", "dve", "pool")}
        self.cnt = {e: 0 for e in self.sem}
        self.seen = {e: {} for e in self.h}
        self.pending = {e: [] for e in self.sem}
        self.bar = es.enter_context(nc.semaphore("s_bar"))
        self.nbar = 0
        self.dsems = [es.enter_context(nc.semaphore(f"s_d{i}")) for i in range(90)]
        self.dval = [0] * len(self.dsems)
        self.dfree = {"pool": list(range(16)), "sp": list(range(16, len(self.dsems)))}
        self.dq = {}
        self.dbufs = []
        self.allbufs = []
        self.pool_out = []

    def buf(self, name, excl=False):
        b = Buf(name, excl)
        self.allbufs.append(b)
        return b

    def _semof(self, key):
        return self.sem[key] if isinstance(key, str) else self.dsems[key[1]]

    def _deps(self, eng, reads, writes):
        deps = {}

        def add(tok):
            if tok is None:
                return
            k, v = tok
            if deps.get(k, 0) < v:
                deps[k] = v

        for b in reads:
            add(b.w)
            if b.excl:
                for k, v in b.r.items():
                    if k != eng:
                        add((k, v))
        for b in writes:
            add(b.w)
            for k, v in b.r.items():
                add((k, v))
        return deps

    def _wait(self, eng, deps):
        seen = self.seen[eng]
        for k, v in deps.items():
            if k == eng and eng == "pe":
                continue
            if seen.get(k, 0) >= v:
                continue
            self.h[eng].wait_ge(self._semof(k), v)
            seen[k] = v

    def _mark(self, tok, reads, writes):
        k, v = tok
        for b in reads:
            if b.r.get(k, 0) < v:
                b.r[k] = v
        for b in writes:
            b.w = tok
            b.r = {}

    def op(self, eng, fn, reads=(), writes=(), signal=True):
        deps = self._deps(eng, reads, writes)
        self._wait(eng, deps)
        ins = fn()
        if not signal:
            self.pending[eng].append((list(reads), list(writes)))
            return None
        ins.then_inc(self.sem[eng], 1)
        self.cnt[eng] += 1
        tok = (eng, self.cnt[eng])
        for r_, w_ in self.pending[eng]:
            self._mark(tok, r_, w_)
        self.pending[eng] = []
        self._mark(tok, reads, writes)
        return tok

    def dma(self, q, out, in_, reads=(), writes=(), sembuf=None):
        deps = self._deps(q, reads, writes)
        self._wait(q, deps)
        if sembuf.sem is None:
            assert self.dfree[q], "out of dma semaphores"
            sembuf.sem = self.dfree[q].pop(0)
            self.dq[sembuf.sem] = q
            self.dbufs.append(sembuf)
        assert self.dq[sembuf.sem] == q, "buffer DMA'd from two queues"
        ins = self.h[q].dma_start(out=out, in_=in_)
        ins.then_inc(self.dsems[sembuf.sem], 16)
        self.dval[sembuf.sem] += 16
        tok = (("d", sembuf.sem), self.dval[sembuf.sem])
        self._mark(tok, reads, writes)
        if q == "pool":
            self.pool_out.append(tok)
            if len(self.pool_out) > 3:
                k, v = self.pool_out.pop(0)
                self._wait("pool", {k: v})
        return tok

    def barrier(self):
        for e in self.pending:
            assert not self.pending[e], f"pending ops on {e} at barrier"
        sp = self.h["sp"]
        seen = self.seen["sp"]
        for e in self.sem:
            if seen.get(e, 0) < self.cnt[e]:
                sp.wait_ge(self.sem[e], self.cnt[e])
                seen[e] = self.cnt[e]
        for b in self.dbufs:
            k = ("d", b.sem)
            if seen.get(k, 0) < self.dval[b.sem]:
                sp.wait_ge(self.dsems[b.sem], self.dval[b.sem])
                seen[k] = self.dval[b.sem]
        sp.sem_inc(self.bar, 1)
        self.nbar += 1
        for e in self.sem:
            self.h[e].wait_ge(self.bar, self.nbar)
        for e in self.h:
            for k2 in self.sem:
                self.seen[e][k2] = self.cnt[k2]
            for i in range(len(self.dsems)):
                self.seen[e][("d", i)] = self.dval[i]
        for b in self.dbufs:
            self.dfree[self.dq[b.sem]].append(b.sem)
            b.sem = None
        self.dbufs = []
        self.pool_out = []
        for b in self.allbufs:
            b.w = None
            b.r = {}
        self.allbufs = [b for b in self.allbufs if b.excl]


def t5_bucket_np(rel):
    rel = np.asarray(rel, np.int64)
    half, max_exact = 16, 8
    ret = (rel > 0).astype(np.int64) * half
    n = np.abs(rel)
    nf = np.maximum(n, 1).astype(np.float32)
    large = max_exact + (np.log(nf / np.float32(max_exact)) / np.float32(math.log(128 / max_exact))
                         * np.float32(half - max_exact)).astype(np.int32)
    large = np.minimum(large, half - 1)
    return ret + np.where(n < max_exact, n, large)


class _Stop(Exception):
    pass


def build(S, L, NSEQ, dbg=False, stop=99):
    nc = bass.Bass("TRN2", target_bir_lowering=False)
    try:
        _build(nc, S, L, NSEQ, dbg, stop)
    except _Stop:
        pass
    return nc


def _build(nc, S, L, NSEQ, dbg, stop):
    NT = S // 128
    NB = S // 512
    SP = S + 30

    def din(name, shape, dt=F32):
        return nc.dram_tensor(name, shape, dt, kind="ExternalInput").ap()

    x_d = din("x", [NSEQ, S, D])
    mem_d = din("mem", [NSEQ, MEM_LEN, D])
    emb_d = din("emb", [128, 2, D])
    btab_d = din("btab", [128, 4, 1152])
    cfar_d = din("cfar", [128, 8])
    win_d = din("w_in", [L, D, IN_W])
    wkv_d = din("w_mem_kv", [L, D, 512])
    wout_d = din("w_out", [L, D, D])
    wup_d = din("w_up", [L, D, D_FF])
    wdn_d = din("w_down", [L, D_FF, D])
    bincol_d = din("bincol", [128, L, 18])
    bv_d = din("bvt", [128, L, 512])
    cwcol_d = din("cwcol", [128, L, 2, CONV_W])
    cvec_d = din("cvec", [128, L, 3, 256])
    lamv_d = din("lamv", [128, L, 4, 64])
    dng_d = din("dng", [128, L, 128])
    vec_d = din("vec", [128, L, 5, D])
    out_d = nc.dram_tensor("out", [NSEQ, S, D], F32, kind="ExternalOutput").ap()

    sk = "ExternalOutput" if dbg else "Internal"

    def dsc(name, shape, dt):
        return nc.dram_tensor(name, shape, dt, kind=sk).ap()

    xres_d = dsc("xres", [NSEQ, S, D], F32)
    x1res_d = dsc("x1res", [NSEQ, S, D], F32)
    xT_d = dsc("xT", [NSEQ, 8, 128, S], BF16)
    x1T_d = dsc("x1T", [NSEQ, 8, 128, S], BF16)
    QT_d = dsc("QT", [NSEQ, 4, 128, S], BF16)
    KT_d = dsc("KT", [NSEQ, 4, 128, S], BF16)
    V_d = dsc("V", [NSEQ, S, 512], BF16)
    QmT_d = dsc("QmT", [NSEQ, 2, 128, S], BF16)
    uT_d = dsc("uT", [NSEQ, 2, 128, SP], BF16)
    cv_d = dsc("cv", [NSEQ, S, 256], BF16)

    def cpt(ap):
        return ap.rearrange("c p t -> p c t")

    with ExitStack() as ges:
        sch = Sched(nc, ges)
        V, G, A, PE = nc.vector, nc.gpsimd, nc.scalar, nc.tensor

        uq = [0]

        def uniq(name):
            uq[0] += 1
            return f"{name}_{uq[0]}"

        def tile(es, name, shape, dt, nb=0, excl=False):
            t = es.enter_context(nc.sbuf_tensor(uniq(name), shape, dt))
            if nb == 0:
                return t, sch.buf(name, excl)
            return t, [sch.buf(f"{name}.{i}", excl) for i in range(nb)]

        psall = ges.enter_context(nc.psum_tensor("psall", [128, 4096], F32))
        PS = [psall[:, i * 512:(i + 1) * 512] for i in range(8)]
        PB = [sch.buf(f"ps{i}", excl=True) for i in range(8)]

        ident_f, identf_b = tile(ges, "ident_f", [128, 128], F32)
        ident, ident_b = tile(ges, "ident", [128, 128], BF16)
        epsc, epsc_b = tile(ges, "epsc", [128, 1], F32)
        zero_bf, zero_b = tile(ges, "zero_bf", [128, 32], BF16)
        scr, scr_b = tile(ges, "scr", [128, 8], F32)

        sch.op("pool", lambda: G.memset(ident_f[:], 1.0), writes=[identf_b])
        sch.op("pool", lambda: G.affine_select(out=ident_f[:], in_=ident_f[:], pattern=[[-1, 128]],
                                               compare_op=ALU.is_equal, fill=0.0, base=0, channel_multiplier=1),
               reads=[identf_b], writes=[identf_b])
        sch.op("dve", lambda: V.tensor_copy(ident[:], ident_f[:]), reads=[identf_b], writes=[ident_b])
        sch.op("dve", lambda: V.memset(epsc[:], LN_EPS), writes=[epsc_b])
        sch.op("dve", lambda: V.memset(zero_bf[:], 0.0), writes=[zero_b])
        for s in range(NSEQ):
            for c in range(2):
                sch.dma("sp", uT_d[s, c, :, 0:15], zero_bf[:, 0:15], reads=[zero_b], sembuf=zero_b)
                sch.dma("sp", uT_d[s, c, :, 15 + S:SP], zero_bf[:, 0:15], reads=[zero_b], sembuf=zero_b)

        class LNCtx:
            def __init__(self, es, N, pfx, nslot=2, inplace=False, xh_eng="act"):
                self.N = N
                self.ns = nslot
                self.k = 0
                self.xh_eng = xh_eng
                nch = max(1, N // 512)
                self.nch = nch
                self.z, self.zb = tile(es, pfx + "z", [128, nslot, N], F32, nslot)
                self.st = es.enter_context(nc.sbuf_tensor(uniq(pfx + "st"), [128, nslot, nch, 6], F32))
                self.stb = [[sch.buf(f"{pfx}st{i}{j}") for j in range(nch)] for i in range(nslot)]
                self.sm, self.smb = tile(es, pfx + "sm", [128, nslot, 8], F32, nslot)
                self.smb2 = [sch.buf(pfx + "sm2") for _ in range(nslot)]
                self.smb3 = [sch.buf(pfx + "sm3") for _ in range(nslot)]
                self.smb4 = [sch.buf(pfx + "sm4") for _ in range(nslot)]
                if inplace:
                    self.xh, self.xhb = self.z, self.zb
                else:
                    self.xh, self.xhb = tile(es, pfx + "xh", [128, nslot, N], F32, nslot)

            def slot(self):
                i = self.k % self.ns
                self.k += 1
                return i

            def stages(self, i, g_ap, b_ap, gb_bufs):
                N = self.N
                z = self.z[:, i, :]
                zb = self.zb[i]
                w = min(N, 512)
                st, sm = self.st, self.sm
                xh = self.xh[:, i, :]
                xhb = self.xhb[i]

                def s1():
                    for j in range(self.nch):
                        sch.op("dve", lambda: V.bn_stats(st[:, i, j, :], z[:, j * w:(j + 1) * w]),
                               reads=[zb], writes=[self.stb[i][j]])
                    sch.op("dve", lambda: V.bn_aggr(sm[:, i, 0:2], st[:, i, :, :]),
                           reads=self.stb[i], writes=[self.smb[i]])

                def s2():
                    sch.op("act", lambda: A.activation(out=sm[:, i, 2:3], in_=sm[:, i, 1:2], func=AF.Ln,
                                                       bias=epsc[:], scale=1.0),
                           reads=[self.smb[i], epsc_b], writes=[self.smb2[i]])
                    sch.op("act", lambda: A.activation(out=sm[:, i, 3:4], in_=sm[:, i, 2:3], func=AF.Exp,
                                                       scale=-0.5),
                           reads=[self.smb2[i]], writes=[self.smb3[i]])

                def s3():
                    sch.op("dve", lambda: V.scalar_tensor_tensor(out=sm[:, i, 4:5], in0=sm[:, i, 0:1], scalar=-1.0,
                                                                 in1=sm[:, i, 3:4], op0=ALU.mult, op1=ALU.mult),
                           reads=[self.smb[i], self.smb3[i]], writes=[self.smb4[i]])
                    if self.xh_eng == "act":
                        sch.op("act", lambda: A.activation(out=xh, in_=z, func=AF.Identity, bias=sm[:, i, 4:5],
                                                           scale=sm[:, i, 3:4]),
                               reads=[zb, self.smb3[i], self.smb4[i]], writes=[xhb])
                    else:
                        sch.op("dve", lambda: V.tensor_scalar(out=xh, in0=z, scalar1=sm[:, i, 3:4],
                                                              scalar2=sm[:, i, 4:5], op0=ALU.mult, op1=ALU.add),
                               reads=[zb, self.smb3[i], self.smb4[i]], writes=[xhb])

                def s4():
                    sch.op("dve", lambda: V.tensor_mul(xh, xh, g_ap), reads=[xhb] + gb_bufs, writes=[xhb])
                    sch.op("pool", lambda: G.tensor_add(xh, xh, b_ap), reads=[xhb] + gb_bufs, writes=[xhb])

                return [s1, s2, s3, s4], xh, xhb

            def run(self, i, g_ap, b_ap, gb_bufs):
                sts, xh, xhb = self.stages(i, g_ap, b_ap, gb_bufs)
                for f in sts:
                    f()
                return xh, xhb

        class XTail:
            def __init__(self, es, pfx, bank, nslot=2, cast_eng="act"):
                self.bank = bank
                self.cast_eng = cast_eng
                self.ns = nslot
                self.xbf, self.xbfb = tile(es, pfx + "xbf", [128, nslot, D], BF16, nslot)
                self.xts, self.xtsb = tile(es, pfx + "xts", [128, nslot, 8, 128], BF16, nslot)
                self.k = 0

            def part1(self, xh, xhb, res_ap):
                i = self.k % self.ns
                self.k += 1
                sch.dma("sp", res_ap, xh, reads=[xhb], sembuf=xhb)
                xbf = self.xbf[:, i, :]
                if self.cast_eng == "act":
                    sch.op("act", lambda: A.copy(out=xbf, in_=xh), reads=[xhb], writes=[self.xbfb[i]])
                else:
                    sch.op("pool", lambda: G.tensor_copy(xbf, xh), reads=[xhb], writes=[self.xbfb[i]])
                return i

            def part2(self, i, T_ap, tok0):
                xbf = self.xbf[:, i, :]
                pst = PS[self.bank][:].bitcast(BF16)
                for c in range(8):
                    sch.op("pe", lambda: PE.transpose(pst[:, c * 128:(c + 1) * 128],
                                                      xbf[:, c * 128:(c + 1) * 128], ident[:]),
                           reads=[self.xbfb[i], ident_b], writes=[PB[self.bank]], signal=(c == 7))
                xts = self.xts[:, i, :, :]
                sch.op("dve", lambda: V.tensor_copy(xts, pst[:, 0:1024].rearrange("p (c t) -> p c t", c=8)),
                       reads=[PB[self.bank]], writes=[self.xtsb[i]])
                sch.dma("sp", cpt(T_ap)[:, :, tok0:tok0 + 128], xts, reads=[self.xtsb[i]], sembuf=self.xtsb[i])

            def run(self, xh, xhb, res_ap, T_ap, tok0):
                i = self.part1(xh, xhb, res_ap)
                self.part2(i, T_ap, tok0)

        with ExitStack() as es:
            embt, embt_b = tile(es, "embt", [128, 2, D], F32)
            sch.dma("sp", embt[:], emb_d, writes=[embt_b], sembuf=embt_b)
            ln = LNCtx(es, D, "l0", nslot=8, inplace=True)
            xt = XTail(es, "l0", 7, nslot=4)
            tiles = [(s, t) for s in range(NSEQ) for t in range(NT)]
            groups = [tiles[i:i + 4] for i in range(0, len(tiles), 4)]

            def ld0(g):
                for j, (s, t) in enumerate(groups[g]):
                    i = (g % 2) * 4 + j
                    sch.dma("sp", ln.z[:, i, :], x_d[s, t * 128:(t + 1) * 128, :], writes=[ln.zb[i]],
                            sembuf=ln.zb[i])

            ld0(0)
            for g, grp in enumerate(groups):
                if g + 1 < len(groups):
                    ld0(g + 1)
                chains = []
                for j, (s, t) in enumerate(grp):
                    i = (g % 2) * 4 + j
                    sts, xh, xhb = ln.stages(i, embt[:, 0, :], embt[:, 1, :], [embt_b])
                    hold = {}

                    def p1(xh=xh, xhb=xhb, s=s, t=t, hold=hold):
                        hold["i"] = xt.part1(xh, xhb, xres_d[s, t * 128:(t + 1) * 128, :])

                    def p2(s=s, t=t, hold=hold):
                        xt.part2(hold["i"], xT_d[s], t * 128)

                    chains.append(sts + [p1, p2])
                for k in range(len(chains[0])):
                    for ch in chains:
                        ch[k]()
            sch.barrier()
            if stop == 0:
                raise _Stop()

        for l in range(L):
            lam_init = 0.8 - 0.6 * math.exp(-0.3 * l)
            last = (l == L - 1)
            if True:
                with ExitStack() as les:
                    bincol, bincol_b = tile(les, "bincol", [128, 18], F32)
                    lamv, lamv_b = tile(les, "lamv", [128, 4, 64], F32)
                    lams, lams_b = tile(les, "lams", [128, 2, 64], F32)
                    lamc, lamc_b = tile(les, "lamc", [128, 8], F32)
                    lamc2_b = sch.buf("lamc2")
                    lamc3_b = sch.buf("lamc3")
                    dng, dng_b = tile(les, "dng", [128, 128], F32)
                    sch.dma("sp", bincol[:], bincol_d[:, l, :], writes=[bincol_b], sembuf=bincol_b)
                    sch.dma("sp", lamv[:], lamv_d[:, l, :, :], writes=[lamv_b], sembuf=lamv_b)
                    sch.dma("sp", dng[:], dng_d[:, l, :], writes=[dng_b], sembuf=dng_b)
                    sch.op("dve", lambda: V.tensor_mul(lams[:], lamv[:, 0:2, :], lamv[:, 2:4, :]),
                           reads=[lamv_b], writes=[lams_b])
                    sch.op("dve", lambda: V.reduce_sum(out=lamc[:, 0:2], in_=lams[:], axis=AX.X),
                           reads=[lams_b], writes=[lamc_b])
                    sch.op("act", lambda: A.activation(out=lamc[:, 2:4], in_=lamc[:, 0:2], func=AF.Exp),
                           reads=[lamc_b], writes=[lamc2_b])
                    sch.op("dve", lambda: V.tensor_sub(lamc[:, 4:5], lamc[:, 2:3], lamc[:, 3:4]),
                           reads=[lamc2_b], writes=[lamc3_b])
                    sch.op("dve", lambda: V.tensor_scalar(out=lamc[:, 5:6], in0=lamc[:, 4:5], scalar1=-1.0,
                                                          scalar2=-lam_init, op0=ALU.mult, op1=ALU.add),
                           reads=[lamc3_b], writes=[lamc3_b])
                    neglam = lamc[:, 5:6]
                    sch.op("dve", lambda: V.tensor_scalar_mul(dng[:], dng[:], 1.0 - lam_init), reads=[dng_b],
                           writes=[dng_b])

                    with ExitStack() as es:
                        win, win_b = tile(es, "win", [128, 8, IN_W], BF16)
                        bvt, bvt_b = tile(es, "bvt", [128, 512], F32)
                        sch.dma("sp", bvt[:], bv_d[:, l, :], writes=[bvt_b], sembuf=bvt_b)
                        win_q = [sch.buf(f"winq{q}") for q in range(3)]
                        for q3 in range(3):
                            for c in range(8):
                                sch.dma("pool", win[:, c, q3 * 768:(q3 + 1) * 768],
                                        win_d[l, c * 128:(c + 1) * 128, q3 * 768:(q3 + 1) * 768],
                                        writes=[win_q[q3]], sembuf=win_q[q3])
                        xb, xb_b = tile(es, "xb", [128, 2, 8, 512], BF16, 2)
                        sig, sig_b = tile(es, "sig", [128, 2, 512], F32, 2)
                        ust, ust_b = tile(es, "ust", [128, 2, 2, 512], BF16, 2)
                        qst, qst_b = tile(es, "qst", [128, 2, 4, 512], BF16, 2)
                        kst, kst_b = tile(es, "kst", [128, 2, 4, 512], BF16, 2)
                        qmst, qmst_b = tile(es, "qmst", [128, 2, 2, 512], BF16, 2)
                        vst, vst_b = tile(es, "vst", [128, 2, 4, 512], BF16, 2)
                        cnt = {"ps": 0, "sig": 0}

                        ablocks = [(s_, tb_) for s_ in range(NSEQ) for tb_ in range(NB)]

                        def ldA(bi):
                            s_, tb_ = ablocks[bi]
                            sch.dma("sp", xb[:, bi % 2, :, :], cpt(xT_d[s_])[:, :, tb_ * 512:(tb_ + 1) * 512],
                                    writes=[xb_b[bi % 2]], sembuf=xb_b[bi % 2])

                        ldA(0)
                        for bi, (s, tb) in enumerate(ablocks):
                            if bi + 1 < len(ablocks):
                                ldA(bi + 1)
                            sl = bi % 2
                            t0 = tb * 512

                            def fm_chunk(cc):
                                bk = cnt["ps"] % 4
                                cnt["ps"] += 1
                                for dc in range(8):
                                    sch.op("pe", lambda: PE.matmul(
                                        PS[bk][:], win[:, dc, cc * 128:(cc + 1) * 128], xb[:, sl, dc, :],
                                        start=(dc == 0), stop=(dc == 7)),
                                        reads=[win_q[cc // 6], xb_b[sl]], writes=[PB[bk]], signal=(dc == 7))
                                return bk

                            for c in range(2):
                                bk = fm_chunk(2 + c)
                                si = cnt["sig"] % 2
                                cnt["sig"] += 1
                                sch.op("act", lambda: A.activation(out=sig[:, si, :], in_=PS[bk][:], func=AF.Sigmoid,
                                                                   bias=bincol[:, 2 + c:3 + c], scale=1.0),
                                       reads=[PB[bk], bincol_b], writes=[sig_b[si]])
                                bk2 = fm_chunk(c)
                                sch.op("dve", lambda: V.scalar_tensor_tensor(
                                    out=ust[:, sl, c, :], in0=PS[bk2][:], scalar=bincol[:, c:c + 1], in1=sig[:, si, :],
                                    op0=ALU.add, op1=ALU.mult),
                                    reads=[PB[bk2], bincol_b, sig_b[si]], writes=[ust_b[sl]])
                            sch.dma("sp", cpt(uT_d[s])[:, :, 15 + t0:15 + t0 + 512], ust[:, sl, :, :],
                                    reads=[ust_b[sl]], sembuf=ust_b[sl])
                            for (dst, dstb, cc0, n, dram) in ((qst, qst_b, 4, 4, QT_d), (kst, kst_b, 8, 4, KT_d),
                                                               (qmst, qmst_b, 16, 2, QmT_d)):
                                for h in range(n):
                                    cc = cc0 + h
                                    bk = fm_chunk(cc)
                                    if h % 2 == 0:
                                        sch.op("act", lambda: A.activation(
                                            out=dst[:, sl, h, :], in_=PS[bk][:], func=AF.Identity,
                                            bias=bincol[:, cc:cc + 1], scale=1.0),
                                            reads=[PB[bk], bincol_b], writes=[dstb[sl]])
                                    else:
                                        sch.op("dve", lambda: V.tensor_scalar_add(
                                            dst[:, sl, h, :], PS[bk][:], bincol[:, cc:cc + 1]),
                                            reads=[PB[bk], bincol_b], writes=[dstb[sl]])
                                sch.dma("sp", cpt(dram[s])[:, :, t0:t0 + 512], dst[:, sl, :, :],
                                        reads=[dstb[sl]], sembuf=dstb[sl])
                            for tt in range(4):
                                bk = 4 + tt
                                for dc in range(8):
                                    sch.op("pe", lambda: PE.matmul(
                                        PS[bk][:], xb[:, sl, dc, tt * 128:(tt + 1) * 128], win[:, dc, 1536:2048],
                                        start=(dc == 0), stop=(dc == 7)),
                                        reads=[win_q[2], xb_b[sl]], writes=[PB[bk]], signal=(dc == 7))
                                sch.op("dve", lambda: V.tensor_add(vst[:, sl, tt, :], PS[bk][:], bvt[:]),
                                       reads=[PB[bk], bvt_b], writes=[vst_b[sl]])
                            sch.dma("sp", V_d[s, t0:t0 + 512, :].rearrange("(tt p) n -> p tt n", p=128),
                                    vst[:, sl, :, :], reads=[vst_b[sl]], sembuf=vst_b[sl])
                        sch.barrier()
                        if stop == 1:
                            raise _Stop()

                    for s in range(NSEQ):
                        with ExitStack() as bes:
                            kmT, kmT_b = tile(bes, "kmT", [128, 2, 256], BF16)
                            vma, vma_b = tile(bes, "vma", [128, 2, 4, 65], BF16)
                            btab, btab_b = tile(bes, "btab", [128, 4, 1152], F32)
                            cfar, cfar_b = tile(bes, "cfar", [128, 8], F32)
                            sch.dma("sp", btab[:], btab_d, writes=[btab_b], sembuf=btab_b)
                            sch.op("act", lambda: A.activation(out=btab[:], in_=btab[:], func=AF.Exp),
                                   reads=[btab_b], writes=[btab_b])
                            sch.dma("sp", cfar[:], cfar_d, writes=[cfar_b], sembuf=cfar_b)
                            KT, KT_b = tile(bes, "KT", [128, 4, S], BF16)
                            VA, VA_b = tile(bes, "VA", [128, NT, 4, 129], BF16)
                            wo, wo_b = tile(bes, "wo", [128, 8, D], BF16)
                            vec, vec_b = tile(bes, "vecB", [128, 3, D], F32)
                            sch.op("pool", lambda: G.memset(VA[:, :, :, 128:129], 1.0), writes=[VA_b])
                            for h in range(4):
                                sch.dma("sp", KT[:, h, :], KT_d[s, h], writes=[KT_b], sembuf=KT_b)
                            for t8 in range(0, NT, 8):
                                n8 = min(8, NT - t8)
                                for h in range(4):
                                    sch.dma("sp", VA[:, t8:t8 + n8, h, 0:128],
                                            V_d[s, t8 * 128:(t8 + n8) * 128, h * 128:(h + 1) * 128].rearrange(
                                                "(t p) d -> p t d", p=128),
                                            writes=[VA_b], sembuf=VA_b)
                            for c in range(8):
                                sch.dma("pool", wo[:, c, :], wout_d[l, c * 128:(c + 1) * 128, :], writes=[wo_b],
                                        sembuf=wo_b)
                            sch.dma("sp", vec[:], vec_d[:, l, 0:3, :], writes=[vec_b], sembuf=vec_b)
                            with ExitStack() as es:
                                wkv, wkv_b = tile(es, "wkv", [128, 8, 512], BF16)
                                cwcol, cwcol_b = tile(es, "cwcol", [128, 2, CONV_W], F32)
                                cvec, cvec_b = tile(es, "cvec", [128, 3, 256], F32)
                                sch.dma("sp", cwcol[:], cwcol_d[:, l, :, :], writes=[cwcol_b], sembuf=cwcol_b)
                                sch.dma("sp", cvec[:], cvec_d[:, l, :, :], writes=[cvec_b], sembuf=cvec_b)
                                for c2 in range(4):
                                    sch.dma("pool", wkv[:, 2 * c2:2 * c2 + 2, :],
                                            wkv_d[l, c2 * 256:(c2 + 1) * 256, :].rearrange("(c p) n -> p c n", p=128),
                                            writes=[wkv_b], sembuf=wkv_b)
                                memb, memb_b = tile(es, "memb", [128, 2, D], BF16)
                                memT, memT_b = tile(es, "memT", [128, 8, 256], BF16)
                                sch.dma("pool", memb[:], mem_d[s].rearrange("(a p) n -> p a n", p=128), writes=[memb_b],
                                        sembuf=memb_b)
                                for a in range(2):
                                    pst = PS[a][:].bitcast(BF16)
                                    for c in range(8):
                                        sch.op("pe", lambda: PE.transpose(
                                            pst[:, c * 128:(c + 1) * 128], memb[:, a, c * 128:(c + 1) * 128], ident[:]),
                                            reads=[memb_b, ident_b], writes=[PB[a]], signal=(c == 7))
                                    sch.op("dve", lambda: V.tensor_copy(
                                        memT[:, :, a * 128:(a + 1) * 128],
                                        pst[:, 0:1024].rearrange("p (c t) -> p c t", c=8)),
                                        reads=[PB[a]], writes=[memT_b])
                                for c in range(2):
                                    bk = 2 + c
                                    for dc in range(8):
                                        sch.op("pe", lambda: PE.matmul(
                                            PS[bk][:, 0:256], wkv[:, dc, c * 128:(c + 1) * 128], memT[:, dc, :],
                                            start=(dc == 0), stop=(dc == 7)),
                                            reads=[wkv_b, memT_b], writes=[PB[bk]], signal=(dc == 7))
                                    sch.op("act", lambda: A.copy(out=kmT[:, c, :], in_=PS[bk][:, 0:256]),
                                           reads=[PB[bk]], writes=[kmT_b])
                                sch.op("pool", lambda: G.memset(vma[:], 1.0), writes=[vma_b])
                                for a in range(2):
                                    bk = 4 + a
                                    for dc in range(8):
                                        sch.op("pe", lambda: PE.matmul(
                                            PS[bk][:, 0:256], memT[:, dc, a * 128:(a + 1) * 128], wkv[:, dc, 256:512],
                                            start=(dc == 0), stop=(dc == 7)),
                                            reads=[wkv_b, memT_b], writes=[PB[bk]], signal=(dc == 7))
                                    sch.op("dve", lambda: V.tensor_copy(
                                        vma[:, a, :, 0:64], PS[bk][:, 0:256].rearrange("p (h d) -> p h d", h=4)),
                                        reads=[PB[bk]], writes=[vma_b])

                                diag, diag_b = tile(es, "diag", [128, 2, CONV_W, 128], BF16)
                                dgb = sch.buf("dg")
                                for c in range(2):
                                    for j in range(CONV_W):
                                        if j % 2:
                                            sch.op("pool", lambda: G.tensor_scalar_mul(
                                                diag[:, c, j, :], ident_f[:], cwcol[:, c, j:j + 1]),
                                                reads=[identf_b, cwcol_b], writes=[sch.buf("dgx")])
                                        else:
                                            sch.op("dve", lambda: V.tensor_scalar_mul(
                                                diag[:, c, j, :], ident_f[:], cwcol[:, c, j:j + 1]),
                                                reads=[identf_b, cwcol_b], writes=[sch.buf("dgx")])
                                sch.op("dve", lambda: V.memset(scr[:, 0:1], 0.0), writes=[diag_b])
                                sch.op("pool", lambda: G.memset(scr[:, 1:2], 0.0), writes=[diag_b])
                                ub, ub_b = tile(es, "ub", [128, 2, 2, 542], BF16, 2)
                                lnc = LNCtx(es, 256, "lc", nslot=8, inplace=True)
                                ex, ex_b = tile(es, "cex", [128, 4, 256], F32, 4)
                                cst, cst_b = tile(es, "cst", [128, 2, 4, 256], BF16, 2)

                                def ldU(tb):
                                    sch.dma("sp", ub[:, tb % 2, :, :], cpt(uT_d[s])[:, :, tb * 512:tb * 512 + 542],
                                            writes=[ub_b[tb % 2]], sembuf=ub_b[tb % 2])

                                def conv_mm(tb):
                                    sl = tb % 2
                                    for tt in range(4):
                                        bk = 4 + tt
                                        for c in range(2):
                                            for j in range(CONV_W):
                                                sch.op("pe", lambda: PE.matmul(
                                                    PS[bk][:, c * 128:(c + 1) * 128],
                                                    ub[:, sl, c, tt * 128 + j:tt * 128 + j + 128], diag[:, c, j, :],
                                                    start=(c == 0 and j == 0), stop=(j == CONV_W - 1),
                                                    skip_group_check=True),
                                                    reads=[ub_b[sl], diag_b], writes=[PB[bk]],
                                                    signal=(c == 1 and j == CONV_W - 1))
                                        i = sl * 4 + tt
                                        sch.op("dve", lambda: V.tensor_add(lnc.z[:, i, :], PS[bk][:, 0:256],
                                                                           cvec[:, 0, :]),
                                               reads=[PB[bk], cvec_b], writes=[lnc.zb[i]])

                                def conv_chains(tb):
                                    sl = tb % 2
                                    chains = []
                                    for tt in range(4):
                                        i = sl * 4 + tt
                                        sts, xh, xhb = lnc.stages(i, cvec[:, 1, :], cvec[:, 2, :], [cvec_b])

                                        def s5(xh=xh, xhb=xhb, tt=tt):
                                            sch.op("act", lambda: A.activation(out=ex[:, tt, :], in_=xh, func=AF.Exp,
                                                                               scale=-1.0),
                                                   reads=[xhb], writes=[ex_b[tt]])

                                        def s6(tt=tt):
                                            sch.op("pool", lambda: G.tensor_scalar_add(ex[:, tt, :], ex[:, tt, :], 1.0),
                                                   reads=[ex_b[tt]], writes=[ex_b[tt]])

                                        def s7(xh=xh, xhb=xhb, tt=tt):
                                            sch.op("dve", lambda: V.reciprocal(ex[:, tt, :], ex[:, tt, :]),
                                                   reads=[ex_b[tt]], writes=[ex_b[tt]])
                                            sch.op("dve", lambda: V.tensor_mul(cst[:, sl, tt, :], xh, ex[:, tt, :]),
                                                   reads=[ex_b[tt], xhb], writes=[cst_b[sl]])

                                        chains.append(sts + [s5, s6, s7])
                                    for k in range(len(chains[0])):
                                        for ch in chains:
                                            ch[k]()
                                    sch.dma("sp", cv_d[s, tb * 512:(tb + 1) * 512, :].rearrange("(tt p) n -> p tt n",
                                                                                                p=128),
                                            cst[:, sl, :, :], reads=[cst_b[sl]], sembuf=cst_b[sl])

                                ldU(0)
                                if NB > 1:
                                    ldU(1)
                                conv_mm(0)
                                for tb in range(NB):
                                    if tb + 1 < NB:
                                        conv_mm(tb + 1)
                                    if tb + 2 < NB:
                                        ldU(tb + 2)
                                    conv_chains(tb)
                                sch.barrier()
                                if stop == 2:
                                    raise _Stop()

                            with ExitStack() as es:
                                qtb, qtb_b = tile(es, "qtb", [128, 2, 4, 512], BF16, 2)
                                qmb, qmb_b = tile(es, "qmb", [128, 2, 2, 512], BF16, 2)
                                mixed = es.enter_context(nc.sbuf_tensor(uniq("mixed"), [128, 4, D], BF16))
                                mx_b = [[sch.buf(f"mx{tt}{c}") for c in range(8)] for tt in range(4)]
                                PT, PT_b = tile(es, "PT", [128, 6, 512], BF16, 6)
                                tmp, tmp_b = tile(es, "tmpb", [128, 4, 512], F32, 4)
                                ep, ep_b = tile(es, "ep", [128, 4, 8], F32, 4)
                                oacc = es.enter_context(nc.sbuf_tensor(uniq("oacc"), [128, 2, 8, 129], F32))
                                oacc_b = [[sch.buf(f"oacc{i}{b}") for b in range(3)] for i in range(2)]
                                ew, ew_b = tile(es, "ew", [128, 2, 4, 128], F32, 2)
                                ew2, ew2_b = tile(es, "ew2", [128, 2, 4, 128], F32, 2)
                                esm, esm_b = tile(es, "esm", [128, 2, 16], F32, 2)
                                esm2_b = [sch.buf("esm2") for _ in range(2)]
                                esm3_b = [sch.buf("esm3") for _ in range(2)]
                                esn, esn_b = tile(es, "esn", [128, 2, 8], F32, 2)
                                pend = []
                                mixT, mixT_b = tile(es, "mixT", [128, 2, 8, 128], BF16, 2)
                                xin, xin_b = tile(es, "xin", [128, 2, D], F32, 2)
                                ln1 = LNCtx(es, D, "l1", nslot=4, inplace=True, xh_eng="dve")
                                xt1 = XTail(es, "l1", 7, cast_eng="pool")
                                bg = []
                                cnt = {"pt": 0, "mpt": 0, "tmp": 0, "tile": 0, "oa": 0}

                                def ldQ(qb):
                                    sl = qb % 2
                                    sch.dma("sp", qtb[:, sl, :, :], cpt(QT_d[s])[:, :, qb * 512:(qb + 1) * 512],
                                            writes=[qtb_b[sl]], sembuf=qtb_b[sl])
                                    sch.dma("sp", qmb[:, sl, :, :], cpt(QmT_d[s])[:, :, qb * 512:(qb + 1) * 512],
                                            writes=[qmb_b[sl]], sembuf=qmb_b[sl])

                                def ldX(k):
                                    xi = k % 2
                                    sch.dma("sp", xin[:, xi, :], xres_d[s, k * 128:(k + 1) * 128, :], writes=[xin_b[xi]],
                                            sembuf=xin_b[xi])

                                ones_bf, ones_b = tile(es, "ones_bf", [1, 128], BF16)
                                bout_bf, bout_b = tile(es, "bout_bf", [1, D], BF16)
                                sch.op("pool", lambda: G.memset(ones_bf[:], 1.0), writes=[ones_b])
                                sch.op("pool", lambda: G.tensor_copy(bout_bf[:], vec[0:1, 0, :]), reads=[vec_b],
                                       writes=[bout_b])
                                ldQ(0)
                                ldX(0)
                                for qb in range(NB):
                                    sl = qb % 2
                                    q0 = qb * 512
                                    if qb + 1 < NB:
                                        ldQ(qb + 1)
                                    for tt in range(4):
                                        sch.dma("sp", mixed[:, tt, 0:256], cv_d[s, q0 + tt * 128:q0 + (tt + 1) * 128, :],
                                                writes=[mx_b[tt][0], mx_b[tt][1]], sembuf=mx_b[tt][0])
                                    if True:
                                        def m_qk(h):
                                            pr = slice((h % 2) * 64, (h % 2) * 64 + 64)
                                            hc = h // 2
                                            b0 = (h % 2) * 2
                                            for mt in range(2):
                                                sch.op("pe", lambda: PE.matmul(
                                                    PS[b0 + mt][:], kmT[pr, hc, mt * 128:(mt + 1) * 128],
                                                    qmb[pr, sl, hc, :], start=True, stop=True),
                                                    reads=[kmT_b, qmb_b[sl]], writes=[PB[b0 + mt]])

                                        def m_ex(h):
                                            b0 = (h % 2) * 2
                                            pi = (cnt["mpt"] % 3) * 2
                                            cnt["mpt"] += 1
                                            sch.op("act", lambda: A.activation(
                                                out=PT[:, pi:pi + 2, :].rearrange("p a b -> p (a b)"),
                                                in_=psall[:, b0 * 512:(b0 + 2) * 512], func=AF.Exp, scale=0.125),
                                                reads=[PB[b0], PB[b0 + 1]], writes=[PT_b[pi], PT_b[pi + 1]])
                                            return [pi, pi + 1]

                                        def m_pv(h, pts):
                                            ab = 4 + (h % 2)
                                            for qt in range(4):
                                                for mt in range(2):
                                                    sch.op("pe", lambda: PE.matmul(
                                                        PS[ab][:, qt * 65:qt * 65 + 65],
                                                        PT[:, pts[mt], qt * 128:(qt + 1) * 128], vma[:, mt, h, :],
                                                        start=(qt == 0 and mt == 0), stop=(mt == 1),
                                                        skip_group_check=True),
                                                        reads=[PT_b[pts[mt]], vma_b], writes=[PB[ab]],
                                                        signal=(qt == 3 and mt == 1))
                                            ei = h
                                            sch.op("dve", lambda: V.reciprocal(
                                                ep[:, ei, 0:4],
                                                PS[ab][:, 0:260].rearrange("p (q e) -> p q e", e=65)[:, :, 64]),
                                                reads=[PB[ab]], writes=[ep_b[ei]])
                                            sch.op("dve", lambda: V.tensor_mul(
                                                mixed[:, :, 768 + h * 64:768 + (h + 1) * 64],
                                                PS[ab][:, 0:260].rearrange("p (q e) -> p q e", e=65)[:, :, 0:64],
                                                ep[:, ei, 0:4].unsqueeze(2).to_broadcast([128, 4, 64])),
                                                reads=[PB[ab], ep_b[ei]], writes=[mx_b[qt][6 + h // 2] for qt in range(4)])

                                        m_qk(0)
                                        for h in range(4):
                                            if h + 1 < 4:
                                                m_qk(h + 1)
                                            pts = m_ex(h)
                                            m_pv(h, pts)
                                    steps = [(h, kt) for h in range(4) for kt in range(NT)]
                                    sched_epi = {}

                                    def qk(i):
                                        h, kt = steps[i]
                                        for m in range(2):
                                            bk = (kt % 2) * 2 + m
                                            pr = slice(m * 64, m * 64 + 64)
                                            sch.op("pe", lambda: PE.matmul(
                                                PS[bk][:], KT[pr, h, kt * 128:(kt + 1) * 128], qtb[pr, sl, h, :],
                                                start=True, stop=True),
                                                reads=[KT_b, qtb_b[sl]], writes=[PB[bk]])

                                    def ex(i):
                                        h, kt = steps[i]
                                        off = kt * 128 - q0
                                        near = (-256 < off < 640)
                                        b0 = (kt % 2) * 2
                                        pi = (cnt["pt"] % 3) * 2
                                        cnt["pt"] += 1
                                        pis = [pi, pi + 1]
                                        pt2 = PT[:, pi:pi + 2, :].rearrange("p a b -> p (a b)")
                                        if near:
                                            ts_ = (cnt["tmp"] % 2) * 2
                                            cnt["tmp"] += 1
                                            sch.op("act", lambda: A.activation(
                                                out=tmp[:, ts_:ts_ + 2, :].rearrange("p a b -> p (a b)"),
                                                in_=psall[:, b0 * 512:(b0 + 2) * 512], func=AF.Exp, scale=0.125),
                                                reads=[PB[b0], PB[b0 + 1]], writes=[tmp_b[ts_], tmp_b[ts_ + 1]])
                                            sch.op("dve", lambda: V.tensor_mul(
                                                PT[:, pi:pi + 2, :], tmp[:, ts_:ts_ + 2, :],
                                                btab[:, h, 512 - off:1024 - off].unsqueeze(1).to_broadcast(
                                                    [128, 2, 512])),
                                                reads=[tmp_b[ts_], tmp_b[ts_ + 1], btab_b],
                                                writes=[PT_b[pi], PT_b[pi + 1]])
                                        else:
                                            ci = h * 2 + (1 if off > 0 else 0)
                                            sch.op("act", lambda: A.activation(
                                                out=pt2, in_=psall[:, b0 * 512:(b0 + 2) * 512], func=AF.Exp,
                                                bias=cfar[:, ci:ci + 1], scale=0.125),
                                                reads=[PB[b0], PB[b0 + 1], cfar_b], writes=[PT_b[pi], PT_b[pi + 1]])
                                        return pis

                                    def pv(i, pis):
                                        h, kt = steps[i]
                                        for m in range(2):
                                            for qt in range(4):
                                                idx = m * 4 + qt
                                                bk = 4 + idx // 3
                                                col = (idx % 3) * 129
                                                sch.op("pe", lambda: PE.matmul(
                                                    PS[bk][:, col:col + 129],
                                                    PT[:, pis[m], qt * 128:(qt + 1) * 128], VA[:, kt, h, :],
                                                    start=(kt == 0 and (idx % 3) == 0), stop=(kt == NT - 1),
                                                    skip_group_check=True),
                                                    reads=[PT_b[pis[m]], VA_b], writes=[PB[bk]],
                                                    signal=(idx % 3 == 2 or idx == 7))

                                    def evac(h):
                                        oi = cnt["oa"] % 2
                                        cnt["oa"] += 1
                                        for b in range(3):
                                            n = 3 if b < 2 else 2
                                            sch.op("dve", lambda: V.tensor_copy(
                                                oacc[:, oi, b * 3:b * 3 + n, :],
                                                PS[4 + b][:, 0:n * 129].rearrange("p (a c) -> p a c", c=129)),
                                                reads=[PB[4 + b]], writes=[oacc_b[oi][b]])
                                        return oi

                                    def epi1(h, oi):
                                        en = "dve" if h == 3 else "pool"
                                        E_ = V if h == 3 else G
                                        sch.op("dve", lambda: V.reciprocal(esm[:, oi, 0:8], oacc[:, oi, :, 128]),
                                               reads=oacc_b[oi], writes=[esm_b[oi]])
                                        sch.op(en, lambda: E_.tensor_scalar_mul(esm[:, oi, 8:12], esm[:, oi, 4:8],
                                                                                   neglam),
                                               reads=[esm_b[oi], lamc3_b], writes=[esm2_b[oi]])
                                        sch.op(en, lambda: E_.tensor_mul(
                                            ew[:, oi, :, :], oacc[:, oi, 0:4, 0:128],
                                            esm[:, oi, 0:4].unsqueeze(2).to_broadcast([128, 4, 128])),
                                            reads=oacc_b[oi] + [esm_b[oi]], writes=[ew_b[oi]])
                                        sch.op(en, lambda: E_.tensor_mul(
                                            ew2[:, oi, :, :], oacc[:, oi, 4:8, 0:128],
                                            esm[:, oi, 8:12].unsqueeze(2).to_broadcast([128, 4, 128])),
                                            reads=oacc_b[oi] + [esm2_b[oi]], writes=[ew2_b[oi]])
                                        sch.op(en, lambda: E_.tensor_add(ew[:, oi, :, :], ew[:, oi, :, :],
                                                                            ew2[:, oi, :, :]),
                                               reads=[ew_b[oi], ew2_b[oi]], writes=[ew_b[oi]])
                                        sch.op(en, lambda: E_.tensor_mul(ew2[:, oi, :, :], ew[:, oi, :, :],
                                                                            ew[:, oi, :, :]),
                                               reads=[ew_b[oi]], writes=[ew2_b[oi]])

                                    def epi1b(h, oi):
                                        sch.op("dve", lambda: V.reduce_sum(out=esm[:, oi, 12:16], in_=ew2[:, oi, :, :],
                                                                           axis=AX.X),
                                               reads=[ew2_b[oi]], writes=[esm3_b[oi]])

                                    def epi2(h, oi):
                                        sch.op("act", lambda: A.activation(
                                            out=esn[:, oi, 0:4], in_=esm[:, oi, 12:16], func=AF.Ln, bias=epsc[:],
                                            scale=1.0 / 128.0),
                                            reads=[esm3_b[oi], epsc_b], writes=[esn_b[oi]])
                                        sch.op("act", lambda: A.activation(
                                            out=esn[:, oi, 4:8], in_=esn[:, oi, 0:4], func=AF.Exp, scale=-0.5),
                                            reads=[esn_b[oi]], writes=[esn_b[oi]])

                                    def epi3(h, oi):
                                        en = "dve" if h == 3 else "pool"
                                        E_ = V if h == 3 else G
                                        sch.op(en, lambda: E_.tensor_mul(
                                            ew[:, oi, :, :], ew[:, oi, :, :],
                                            esn[:, oi, 4:8].unsqueeze(2).to_broadcast([128, 4, 128])),
                                            reads=[ew_b[oi], esn_b[oi]], writes=[ew_b[oi]])
                                        sch.op(en, lambda: E_.tensor_mul(
                                            mixed[:, :, 256 + h * 128:256 + (h + 1) * 128], ew[:, oi, :, :],
                                            dng[:].unsqueeze(1).to_broadcast([128, 4, 128])),
                                            reads=[ew_b[oi], dng_b], writes=[mx_b[qt][2 + h] for qt in range(4)])

                                    def run_sched(i):
                                        for fn in sched_epi.pop(i, []):
                                            fn()

                                    if True:
                                        def head_end(i):
                                            h = steps[i][0]
                                            for j in sorted(sched_epi):
                                                for fn in sched_epi[j]:
                                                    fn()
                                            sched_epi.clear()
                                            oi = evac(h)
                                            sched_epi.setdefault(i + 3, []).append(lambda h=h, oi=oi: epi1(h, oi))
                                            sched_epi.setdefault(i + 11, []).append(lambda h=h, oi=oi: epi1b(h, oi))
                                            sched_epi.setdefault(i + 14, []).append(lambda h=h, oi=oi: epi2(h, oi))
                                            sched_epi.setdefault(i + 17, []).append(lambda h=h, oi=oi: epi3(h, oi))

                                        qk(0)
                                        prev = None
                                        for i, (h, kt) in enumerate(steps):
                                            if i + 1 < len(steps):
                                                qk(i + 1)
                                            pis = ex(i)
                                            if prev is not None:
                                                pv(i - 1, prev)
                                                if steps[i - 1][1] == NT - 1:
                                                    head_end(i - 1)
                                            prev = pis
                                            if i >= 2 and (i % 2 == 0) and bg:
                                                bg.pop(0)()
                                            run_sched(i)
                                        pv(len(steps) - 1, prev)
                                        head_end(len(steps) - 1)
                                        for i in sorted(sched_epi):
                                            for fn in sched_epi[i]:
                                                fn()
                                        sched_epi.clear()
                                    for fn in pend:
                                        fn()
                                    pend.clear()
                                    pst = PS[7][:].bitcast(BF16)

                                    def mixT_of(tt, k):
                                        mi = k % 2
                                        for c in range(8):
                                            sch.op("pe", lambda: PE.transpose(
                                                pst[:, c * 128:(c + 1) * 128], mixed[:, tt, c * 128:(c + 1) * 128],
                                                ident[:]),
                                                reads=[mx_b[tt][c], ident_b], writes=[PB[7]], signal=(c == 7))
                                        sch.op("dve", lambda: V.tensor_copy(
                                            out=mixT[:, mi, :, :], in_=pst[:, 0:1024].rearrange("p (c t) -> p c t", c=8)),
                                            reads=[PB[7]], writes=[mixT_b[mi]])

                                    if True:
                                        for fn in bg:
                                            fn()
                                        bg.clear()
                                        mixT_of(0, cnt["tile"])
                                        chains = []
                                        for tt in range(4):
                                            k = cnt["tile"]
                                            cnt["tile"] += 1
                                            mi = k % 2
                                            xi = k % 2
                                            tok0 = q0 + tt * 128
                                            if k + 1 < NT:
                                                ldX(k + 1)
                                            if tt + 1 < 4:
                                                mixT_of(tt + 1, k + 1)
                                            zi = ln1.slot()
                                            for hf in range(2):
                                                bk = (tt % 3) * 2 + hf
                                                sch.op("pe", lambda: PE.matmul(
                                                    PS[bk][:], ones_bf[:], bout_bf[:, hf * 512:(hf + 1) * 512],
                                                    start=True, stop=False),
                                                    reads=[ones_b, bout_b], writes=[PB[bk]], signal=False)
                                                for c in range(8):
                                                    sch.op("pe", lambda: PE.matmul(
                                                        PS[bk][:], mixT[:, mi, c, :], wo[:, c, hf * 512:(hf + 1) * 512],
                                                        start=False, stop=(c == 7)),
                                                        reads=[mixT_b[mi], wo_b], writes=[PB[bk]], signal=(c == 7))
                                                sch.op("dve", lambda: V.scalar_tensor_tensor(
                                                    out=ln1.z[:, zi, hf * 512:(hf + 1) * 512],
                                                    in0=xin[:, xi, hf * 512:(hf + 1) * 512], scalar=ALPHA, in1=PS[bk][:],
                                                    op0=ALU.mult, op1=ALU.add),
                                                    reads=[PB[bk], xin_b[xi]], writes=[ln1.zb[zi]])
                                            sts, xh, xhb = ln1.stages(zi, vec[:, 1, :], vec[:, 2, :], [vec_b])
                                            hold = {}

                                            def p1(xh=xh, xhb=xhb, tok0=tok0, hold=hold):
                                                hold["i"] = xt1.part1(xh, xhb, x1res_d[s, tok0:tok0 + 128, :])

                                            def p2(tok0=tok0, hold=hold):
                                                xt1.part2(hold["i"], x1T_d[s], tok0)

                                            chains.append(sts + [p1, p2])
                                        for a, b in ((0, 1), (2, 3)):
                                            for fa, fb in zip(chains[a], chains[b]):
                                                bg.append(fa)
                                                bg.append(fb)
                                for fn in bg:
                                    fn()
                                bg.clear()
                                for fn in pend:
                                    fn()
                                pend.clear()
                                sch.barrier()
                                if stop == 3:
                                    raise _Stop()

            with ExitStack() as es:
                wup, wup_b = tile(es, "wup", [128, 8, D_FF], BF16)
                wdn, wdn_b = tile(es, "wdn", [128, 32, D], BF16)
                vec, vec_b = tile(es, "vecE", [128, 2, D], F32)
                wup_q = [sch.buf(f"wupq{q}") for q in range(4)]
                wdn_q = [sch.buf(f"wdnq{q}") for q in range(4)]
                for q4 in range(4):
                    for c in range(8):
                        sch.dma("pool", wup[:, c, q4 * 1024:(q4 + 1) * 1024],
                                wup_d[l, c * 128:(c + 1) * 128, q4 * 1024:(q4 + 1) * 1024], writes=[wup_q[q4]],
                                sembuf=wup_q[q4])
                for a in range(32):
                    sch.dma("pool", wdn[:, a, :], wdn_d[l, a * 128:(a + 1) * 128, :], writes=[wdn_q[a // 8]],
                            sembuf=wdn_q[a // 8])
                sch.dma("sp", vec[:], vec_d[:, l, 3:5, :], writes=[vec_b], sembuf=vec_b)
                xb, xb_b = tile(es, "xbE", [128, 8, 512], BF16)
                hT = es.enter_context(nc.sbuf_tensor(uniq("hT"), [128, 32, 512], BF16))
                hT_b = [sch.buf(f"hT{f}") for f in range(32)]
                rl, rl_b = tile(es, "rl", [128, 2, 512], F32, 2)
                xin, xin_b = tile(es, "xinE", [128, 1, D], F32, 1)
                ln2 = LNCtx(es, D, "l2", nslot=1)
                xt2 = XTail(es, "l2", 3, nslot=1)
                cnt = {"up": 0}
                pend = []
                blocks = [(s, tb) for s in range(NSEQ) for tb in range(NB)]

                def ldE(bi):
                    s_, tb_ = blocks[bi]
                    sch.dma("sp", xb[:], cpt(x1T_d[s_])[:, :, tb_ * 512:(tb_ + 1) * 512], writes=[xb_b], sembuf=xb_b)

                ldE(0)
                for bi, (s, tb) in enumerate(blocks):
                    t0 = tb * 512
                    for f in range(32):
                        bk = cnt["up"] % 3
                        ri = cnt["up"] % 2
                        cnt["up"] += 1
                        for dc in range(8):
                            sch.op("pe", lambda: PE.matmul(
                                PS[bk][:], wup[:, dc, f * 128:(f + 1) * 128], xb[:, dc, :],
                                start=(dc == 0), stop=(dc == 7)),
                                reads=[wup_q[f // 8], xb_b], writes=[PB[bk]], signal=(dc == 7))
                        sch.op("act", lambda: A.activation(out=rl[:, ri, :], in_=PS[bk][:], func=AF.Relu),
                               reads=[PB[bk]], writes=[rl_b[ri]])
                        if f % 4:
                            sch.op("dve", lambda: V.tensor_mul(hT[:, f, :], rl[:, ri, :], rl[:, ri, :]),
                                   reads=[rl_b[ri]], writes=[hT_b[f]])
                        else:
                            sch.op("pool", lambda: G.tensor_mul(hT[:, f, :], rl[:, ri, :], rl[:, ri, :]),
                                   reads=[rl_b[ri]], writes=[hT_b[f]])
                        if f == 6:
                            for fn in pend:
                                fn()
                            pend.clear()
                    if bi + 1 < len(blocks):
                        ldE(bi + 1)
                    for tt in range(4):
                        tok0 = t0 + tt * 128
                        sch.dma("sp", xin[:, 0, :], x1res_d[s, tok0:tok0 + 128, :], writes=[xin_b[0]],
                                sembuf=xin_b[0])
                        zi = ln2.slot()
                        for hf in range(2):
                            bk = 4 + (tt % 2) * 2 + hf
                            for f in range(32):
                                sch.op("pe", lambda: PE.matmul(
                                    PS[bk][:], hT[:, f, tt * 128:(tt + 1) * 128],
                                    wdn[:, f, hf * 512:(hf + 1) * 512], start=(f == 0), stop=(f == 31)),
                                    reads=[hT_b[f], wdn_q[f // 8]], writes=[PB[bk]], signal=(f == 31))
                            sch.op("dve", lambda: V.scalar_tensor_tensor(
                                out=ln2.z[:, zi, hf * 512:(hf + 1) * 512],
                                in0=xin[:, 0, hf * 512:(hf + 1) * 512], scalar=ALPHA, in1=PS[bk][:],
                                op0=ALU.mult, op1=ALU.add),
                                reads=[PB[bk], xin_b[0]], writes=[ln2.zb[zi]])
                        for fn in pend:
                            fn()
                        pend.clear()
                        xh, xhb = ln2.run(zi, vec[:, 0, :], vec[:, 1, :], [vec_b])
                        dst = out_d if last else xres_d
                        xi2 = xt2.part1(xh, xhb, dst[s, tok0:tok0 + 128, :])
                        if not last:
                            pend.append(lambda xi2=xi2, s=s, tok0=tok0: xt2.part2(xi2, xT_d[s], tok0))
                for fn in pend:
                    fn()
                pend.clear()
                sch.barrier()
                if stop == 4:
                    raise _Stop()
    return nc


def host_inputs(S, L, NSEQ, inp, core):
    f = np.float32
    b0 = core * NSEQ
    rel = (np.arange(128)[:, None] - np.arange(1152)[None, :] + 512)
    bk = t5_bucket_np(rel)
    rb = np.asarray(inp["rel_bias"], f)
    btab = np.ascontiguousarray(rb[bk].transpose(0, 2, 1))
    cfar = np.empty((128, 8), f)
    for h in range(4):
        cfar[:, 2 * h] = rb[15, h]
        cfar[:, 2 * h + 1] = rb[31, h]
    rep = lambda a: np.ascontiguousarray(np.broadcast_to(np.asarray(a, f)[None], (128,) + tuple(np.shape(a))))
    b_in = np.asarray(inp["b_in"], f)[:L]
    m = {
        "x": np.ascontiguousarray(np.asarray(inp["x"], f)[b0:b0 + NSEQ]),
        "mem": np.ascontiguousarray(np.asarray(inp["mem"], f)[b0:b0 + NSEQ]),
        "emb": rep(np.stack([inp["emb_ln_g"], inp["emb_ln_b"]])),
        "btab": btab, "cfar": cfar,
        "w_in": np.ascontiguousarray(np.asarray(inp["w_in"], f)[:L]),
        "w_mem_kv": np.ascontiguousarray(np.asarray(inp["w_mem_kv"], f)[:L]),
        "w_out": np.ascontiguousarray(np.asarray(inp["w_out"], f)[:L]),
        "w_up": np.ascontiguousarray(np.asarray(inp["w_up"], f)[:L]),
        "w_down": np.ascontiguousarray(np.asarray(inp["w_down"], f)[:L]),
        "bincol": np.ascontiguousarray(b_in.reshape(L, 18, 128).transpose(2, 0, 1)),
        "bvt": rep(b_in[:, 1536:2048]),
        "cwcol": np.ascontiguousarray(np.asarray(inp["conv_w"], f)[:L].reshape(L, CONV_W, 2, 128).transpose(3, 0, 2, 1)),
        "cvec": rep(np.stack([np.asarray(inp["conv_b"], f)[:L], np.asarray(inp["conv_ln_g"], f)[:L],
                              np.asarray(inp["conv_ln_b"], f)[:L]], axis=1)),
        "lamv": rep(np.stack([np.asarray(inp["lambda_q1"], f)[:L], np.asarray(inp["lambda_q2"], f)[:L],
                              np.asarray(inp["lambda_k1"], f)[:L], np.asarray(inp["lambda_k2"], f)[:L]], axis=1)),
        "dng": rep(np.asarray(inp["diff_norm_g"], f)[:L]),
        "vec": rep(np.stack([np.asarray(inp["b_out"], f)[:L], np.asarray(inp["ln1_g"], f)[:L],
                             np.asarray(inp["ln1_b"], f)[:L], np.asarray(inp["ln2_g"], f)[:L],
                             np.asarray(inp["ln2_b"], f)[:L]], axis=1)),
    }
    return m


_NC_CACHE = {}


def kernel(**inputs):
    x = np.asarray(inputs["x"])
    B, S, _ = x.shape
    L = DEPTH
    NSEQ = B // NCORES
    key = (S, L, NSEQ)
    if key not in _NC_CACHE:
        _NC_CACHE[key] = build(S, L, NSEQ)
    nc = _NC_CACHE[key]
    in_maps = [host_inputs(S, L, NSEQ, inputs, c) for c in range(NCORES)]
    res = run_bass_kernel_spmd(nc, in_maps, core_ids=list(range(NCORES)))
    out = np.concatenate([np.asarray(r["out"]) for r in res.results], axis=0)
    return out.astype(np.float32)
```
